# Optimizing a Trainium2 kernel written in Bass

```python
import math
import jax
import jax.numpy as jnp
from jax import lax
import numpy as np

D_MODEL = 1024
BATCH = 8
SEQ = 4096
DEPTH = 2

N_MIXERS = 2
N_POOL_LAYERS = (DEPTH + 1) // 2
N_ATTN_LAYERS = DEPTH // 2

POOL_WINDOWS = (2, 4, 8, 16)
N_POOL_GROUPS = len(POOL_WINDOWS)
POOL_GC = D_MODEL // N_POOL_GROUPS

HEAD_DIM = 64
N_HEADS = D_MODEL // HEAD_DIM
N_KV_HEADS = 4
N_GROUPS = N_HEADS // N_KV_HEADS
WINDOW = 128
BLOCK = 128
QKV_DIM = (N_HEADS + 2 * N_KV_HEADS) * HEAD_DIM

D_FF = 2816
CONV_W = 3

RMS_EPS = 1e-6

kernel_name = "hybrid_pool_swa_sink_alibi_convffn"


def rms_norm(x, g):
    xf = x.astype(jnp.float32)
    y = xf * lax.rsqrt(jnp.mean(xf * xf, axis=-1, keepdims=True) + RMS_EPS)
    return (y * g.astype(jnp.float32)).astype(x.dtype)


def alibi_slopes(n_heads):
    h = jnp.arange(1, n_heads + 1, dtype=jnp.float32)
    return jnp.exp2(-8.0 / n_heads * h)


def pool_mixer(h, w, b, scale):
    B, S, D = h.shape
    hf = h.astype(jnp.float32)
    csum = jnp.cumsum(hf, axis=1)
    t1 = jnp.arange(1, S + 1)
    outs = []
    for gi, win in enumerate(POOL_WINDOWS):
        sl = slice(gi * POOL_GC, (gi + 1) * POOL_GC)
        c = csum[..., sl]
        lo = jnp.pad(c, ((0, 0), (win, 0), (0, 0)))[:, :S]
        count = jnp.minimum(t1, win).astype(jnp.float32)
        mean = (c - lo) / count[None, :, None]
        outs.append(mean - hf[..., sl])
    p = jnp.stack(outs, axis=2).astype(h.dtype)
    y = jnp.einsum('bsgc,gcd->bsgd', p, w) + b
    return y.reshape(B, S, D) * scale


def swa_sink_attention(h, w_qkv, b_qkv, sinks, w_o):
    B, S, _ = h.shape
    nb = S // BLOCK
    qkv = h @ w_qkv + b_qkv
    q = qkv[..., :N_HEADS * HEAD_DIM].reshape(B, nb, BLOCK, N_KV_HEADS, N_GROUPS, HEAD_DIM)
    k = qkv[..., N_HEADS * HEAD_DIM:(N_HEADS + N_KV_HEADS) * HEAD_DIM].reshape(B, S, N_KV_HEADS, HEAD_DIM)
    v = qkv[..., (N_HEADS + N_KV_HEADS) * HEAD_DIM:].reshape(B, S, N_KV_HEADS, HEAD_DIM)

    def band(t):
        tp = jnp.pad(t, ((0, 0), (BLOCK, 0), (0, 0), (0, 0)))[:, :S]
        prev = tp.reshape(B, nb, BLOCK, N_KV_HEADS, HEAD_DIM)
        cur = t.reshape(B, nb, BLOCK, N_KV_HEADS, HEAD_DIM)
        return jnp.concatenate([prev, cur], axis=2)

    kb, vb = band(k), band(v)
    scores = jnp.einsum('bnqkgd,bnskd->bnkgqs', q, kb).astype(jnp.float32) * (HEAD_DIM ** -0.5)

    qi = jnp.arange(BLOCK)[:, None]
    kj = jnp.arange(2 * BLOCK)[None, :]
    dist = qi + BLOCK - kj
    key_pos = jnp.arange(nb)[:, None] * BLOCK - BLOCK + jnp.arange(2 * BLOCK)[None, :]
    mask = ((dist >= 0) & (dist < WINDOW))[None] & (key_pos >= 0)[:, None, :]

    slopes = alibi_slopes(N_HEADS).reshape(N_KV_HEADS, N_GROUPS)
    bias = -slopes[:, :, None, None] * dist.astype(jnp.float32)[None, None]
    scores = jnp.where(mask[None, :, None, None], scores + bias, -jnp.inf)

    sink = sinks.astype(jnp.float32).reshape(N_KV_HEADS, N_GROUPS)[None, None, :, :, None, None]
    m = jnp.maximum(jnp.max(scores, axis=-1, keepdims=True), sink)
    p = jnp.exp(scores - m)
    denom = jnp.sum(p, axis=-1, keepdims=True) + jnp.exp(sink - m)
    probs = (p / denom).astype(v.dtype)
    o = jnp.einsum('bnkgqs,bnskd->bnqkgd', probs, vb).reshape(B, S, N_HEADS * HEAD_DIM)
    return o @ w_o


def conv_glu_ffn(h, w_in, conv_w, conv_b, w_out):
    u = h @ w_in
    c = lax.conv_general_dilated(
        u, conv_w[:, None, :].astype(u.dtype), window_strides=(1,),
        padding=[(CONV_W - 1, 0)], dimension_numbers=('NWC', 'WIO', 'NWC'),
        feature_group_count=2 * D_FF) + conv_b
    a, g = c[..., :D_FF], c[..., D_FF:]
    return (a * jax.nn.silu(g)) @ w_out


def setup_inputs(seed: int = 0) -> dict:
    key = jax.random.key(seed)
    ks = jax.random.split(key, 16)
    f32 = jnp.float32
    nrm = lambda k, shape, s: jax.random.normal(k, shape, f32) * s
    return {
        "x": nrm(ks[0], (BATCH, SEQ, D_MODEL), 1.0),
        "pool_w": nrm(ks[1], (N_POOL_LAYERS, N_POOL_GROUPS, POOL_GC, POOL_GC), POOL_GC ** -0.5),
        "pool_b": nrm(ks[2], (N_POOL_LAYERS, N_POOL_GROUPS, POOL_GC), 0.02),
        "pool_scale": 1.0 + nrm(ks[3], (N_POOL_LAYERS, D_MODEL), 0.05),
        "attn_w_qkv": nrm(ks[4], (N_ATTN_LAYERS, D_MODEL, QKV_DIM), D_MODEL ** -0.5),
        "attn_b_qkv": nrm(ks[5], (N_ATTN_LAYERS, QKV_DIM), 0.02),
        "attn_sinks": nrm(ks[6], (N_ATTN_LAYERS, N_HEADS), 0.5),
        "attn_w_o": nrm(ks[7], (N_ATTN_LAYERS, N_HEADS * HEAD_DIM, D_MODEL), (N_HEADS * HEAD_DIM) ** -0.5),
        "norm_mix": 1.0 + nrm(ks[8], (DEPTH, D_MODEL), 0.05),
        "norm_ffn": 1.0 + nrm(ks[9], (DEPTH, D_MODEL), 0.05),
        "ffn_w_in": nrm(ks[10], (DEPTH, D_MODEL, 2 * D_FF), D_MODEL ** -0.5),
        "ffn_conv_w": nrm(ks[11], (DEPTH, CONV_W, 2 * D_FF), CONV_W ** -0.5),
        "ffn_conv_b": nrm(ks[12], (DEPTH, 2 * D_FF), 0.02),
        "ffn_w_out": nrm(ks[13], (DEPTH, D_FF, D_MODEL), D_FF ** -0.5),
        "norm_f": 1.0 + nrm(ks[14], (D_MODEL,), 0.05),
    }


def reference(x, pool_w, pool_b, pool_scale, attn_w_qkv, attn_b_qkv, attn_sinks, attn_w_o,
              norm_mix, norm_ffn, ffn_w_in, ffn_conv_w, ffn_conv_b, ffn_w_out, norm_f):
    h = x
    for i in range(DEPTH):
        hn = rms_norm(h, norm_mix[i])
        j = i // N_MIXERS
        if i % N_MIXERS == 0:
            mix = pool_mixer(hn, pool_w[j], pool_b[j], pool_scale[j])
        else:
            mix = swa_sink_attention(hn, attn_w_qkv[j], attn_b_qkv[j], attn_sinks[j], attn_w_o[j])
        h = h + mix
        hn = rms_norm(h, norm_ffn[i])
        h = h + conv_glu_ffn(hn, ffn_w_in[i], ffn_conv_w[i], ffn_conv_b[i], ffn_w_out[i])
    return rms_norm(h, norm_f)
```

```python
import numpy as np
from contextlib import ExitStack
import concourse.bass as bass
import concourse.mybir as mybir
from concourse.bass_utils import run_bass_kernel_spmd

F32 = mybir.dt.float32
BF16 = mybir.dt.bfloat16
AF = mybir.ActivationFunctionType
ALU = mybir.AluOpType

D = 1024
KC = 8
T = 512
FF = 2816
FC = 22
NSLOT = 4
SLOTC = 4096
NSCR = 6
EPS = 1e-6

PASS_PIECES = []
_off = 0
POOLW_OFF = 0
_off += 2048
SEG_BOUNDS = [(0, 2048)]
for _l in range(2):
    if _l == 1:
        _s = _off
        for _n in (4096, 4096, 4096, 2048, 4096, 4096):
            PASS_PIECES.append((_off, _n, len(SEG_BOUNDS)))
            _off += _n
        SEG_BOUNDS.append((_s, _off))
    _s = _off
    for _j in range(11):
        PASS_PIECES.append((_off, 4096, len(SEG_BOUNDS)))
        _off += 4096
    SEG_BOUNDS.append((_s, _off))
    _s = _off
    for _d in range(8):
        PASS_PIECES.append((_off, 2816, len(SEG_BOUNDS)))
        _off += 2816
    SEG_BOUNDS.append((_s, _off))
NW = _off
NPP = len(PASS_PIECES)

P_G = 0
P_PB = 40
P_PS = 48
P_BQ = 56
P_BK = 64
P_SK = 68
P_CW = 76
P_CB = P_CW + 264
P_BV = P_CB + 88
P_IC = P_BV + 256
P_PBS = P_IC + 64
P_SE = P_PBS + 8
NP = P_SE + 8


class SemObj:
    def __init__(self, sem, h=None):
        self.sem = sem
        self.h = h
        self.cnt = 0
        self.waited = {}


class Tracker:
    def __init__(self):
        self.lastw = {}
        self.readers = {}

    def _deps(self, reads, writes):
        deps = {}

        def add(so, v):
            if deps.get(so, 0) < v:
                deps[so] = v
        for k in reads:
            lw = self.lastw.get(k)
            if lw:
                add(*lw)
        for k in writes:
            lw = self.lastw.get(k)
            if lw:
                add(*lw)
            for so, v in self.readers.get(k, {}).items():
                add(so, v)
        return deps

    def _wait(self, eng, deps, skip_self):
        for so, v in deps.items():
            if so is eng and skip_self:
                continue
            if eng.waited.get(so, 0) >= v:
                continue
            eng.h.wait_ge(so.sem, v)
            eng.waited[so] = v

    def _record(self, so, val, reads, writes):
        for k in reads:
            self.readers.setdefault(k, {})[so] = val
        for k in writes:
            self.lastw[k] = (so, val)
            self.readers[k] = {}

    def emit(self, eng, fn, reads=(), writes=(), signal=True, skip_self=False):
        self._wait(eng, self._deps(reads, writes), skip_self)
        ins = fn()
        if signal:
            ins.then_inc(eng.sem, 1)
            eng.cnt += 1
            val = eng.cnt
        else:
            val = eng.cnt + 1
        self._record(eng, val, reads, writes)
        return ins

    def dma(self, q, dsem, out, in_, reads=(), writes=()):
        self._wait(q, self._deps(reads, writes), True)
        q.h.dma_start(out=out, in_=in_).then_inc(dsem.sem, 16)
        dsem.cnt += 16
        self._record(dsem, dsem.cnt, reads, writes)


def build(S, stop_after="final"):
    NT = S // T
    nc = bass.Bass("TRN2", target_bir_lowering=False)
    x_d = nc.dram_tensor("x", [S, D], F32, kind="ExternalInput").ap()
    wall_d = nc.dram_tensor("wall", [128, NW], F32, kind="ExternalInput").ap()
    par_d = nc.dram_tensor("par", [128, NP], F32, kind="ExternalInput").ap()
    etab_d = nc.dram_tensor("etab", [128, 4096], F32, kind="ExternalInput").ap()
    id_d = nc.dram_tensor("ident", [128, 128], F32, kind="ExternalInput").ap()
    out_d = nc.dram_tensor("out", [S, D], F32, kind="ExternalOutput").ap()
    wbf_d = nc.dram_tensor("wbf", [128, NW], BF16).ap()

    with ExitStack() as es:
        def sb(name, shape, dt):
            return es.enter_context(nc.sbuf_tensor(name, shape, dt))

        def sem(name):
            return SemObj(es.enter_context(nc.semaphore(name)))

        def esem(name, h):
            so = sem(name)
            so.h = h
            return so

        PE = esem("pe", nc.tensor)
        ACT = esem("act", nc.scalar)
        DVE = esem("dve", nc.vector)
        POOL = esem("pool", nc.gpsimd)
        SP = esem("sp", nc.sync)
        d_c = [sem(f"dc{i}") for i in range(3)]
        d_seg = [sem(f"dseg{i}") for i in range(len(SEG_BOUNDS))]
        d_w = [sem(f"dw{i}") for i in range(NSLOT)]
        d_pw = sem("dpw")
        d_in = [sem(f"din{i}") for i in range(4)]
        d_out = [sem(f"dout{i}") for i in range(2)]
        tr = Tracker()

        def E(eng, fn, r=(), w=(), sig=True):
            return tr.emit(eng, fn, r, w, sig, skip_self=(eng is PE))

        ident = sb("ident_s", [128, 128], F32)
        par = sb("par_s", [128, NP], F32)
        etab = sb("etab_s", [128, 4096], BF16)
        ones_bf = sb("ones_bf", [128, 128], BF16)
        epst = sb("epst", [128, 8], F32)
        onespad = sb("onespad", [128, 192], BF16)
        poolw = sb("poolw", [128, 2048], BF16)
        wslot = [sb(f"wslot{i}", [128, SLOTC], BF16) for i in range(NSLOT)]
        xin = [sb(f"xin{i}", [128, 1024], F32) for i in range(4)]
        xout = [sb(f"xout{i}", [128, 1024], F32) for i in range(2)]
        h = sb("h", [128, 8, 512], F32)
        hn = sb("hn", [128, 8, 512], BF16)
        sqb = [sb(f"sq{i}", [128, 512], BF16) for i in range(3)]
        actb = sb("actb", [128, FC * 512], BF16)
        hfv = actb[:, 0:8192].bitcast(F32)

        class _A:
            def __getitem__(self, key):
                p, c, cols = key
                return actb[p, c * 512:(c + 1) * 512]
        act = _A()

        def hf(c):
            return hfv[:, c * 512:(c + 1) * 512]
        qo = sb("qo", [128, 8192], BF16)
        hxv = qo[:, :].bitcast(F32)

        class _V:
            def __init__(self, base):
                self.base = base

            def __getitem__(self, key):
                p, c, cols = key
                lo = 0 if cols.start is None else cols.start
                hi = 512 if cols.stop is None else cols.stop
                return qo[p, self.base + c * 512 + lo: self.base + c * 512 + hi]
        qT = _V(0)
        oT = _V(4096)

        def hx(c):
            return hxv[:, c * 512:(c + 1) * 512]

        def hx_alias(c):
            return [("q", 2 * c), ("q", 2 * c + 1)] if c < 4 else [("o", 2 * (c - 4)), ("o", 2 * (c - 4) + 1)]
        kTlo = sb("kTlo", [128, 4, 1024], BF16)
        kThi = sb("kThi", [128, 4, 1024], BF16)
        vpad = sb("vpad", [128, 5, 4, 192], BF16)
        PT = [sb(f"PT{i}", [128, 4, 512], BF16) for i in range(2)]
        SCR = [sb(f"scr{i}", [128, 528], F32) for i in range(NSCR + 4)]
        halo = sb("halo", [128, 8, 16], F32)
        carry = sb("carry", [128, 88, 2], F32)
        UB = [sb(f"ub{i}", [128, 514], F32) for i in range(4)]
        ps = [es.enter_context(nc.psum_tensor(f"ps{i}", [128, 512], F32)) for i in range(8)]

        st = {"bank": 0, "scr": 0, "sq": 0, "use": 0, "load": 0, "rstd": 0, "ubk": 0}

        def nb():
            b = st["bank"] % 7
            st["bank"] += 1
            return b

        def scr():
            i = st["scr"] % NSCR
            st["scr"] += 1
            return i

        def pc(col):
            return par[:, col:col + 1]

        tr.dma(SP, d_c[0], ident[:, :], id_d[:, :], writes=["ident"])
        tr.dma(SP, d_c[1], par[:, :], par_d[:, :], writes=["par"])
        tr.dma(POOL, d_c[2], etab[:, :], etab_d[:, :], writes=["etab"])
        for si, (a, b) in enumerate(SEG_BOUNDS):
            for a2 in range(a, b, 8192):
                b2 = min(b, a2 + 8192)
                tr.dma(POOL, d_seg[si], wbf_d[:, a2:b2], wall_d[:, a2:b2])
            tr.lastw[("wbf", si)] = (d_seg[si], d_seg[si].cnt)
        tr.dma(SP, d_pw, poolw[:, :], wbf_d[:, 0:2048], reads=[("wbf", 0)], writes=["poolw"])

        E(DVE, lambda: nc.vector.memset(ones_bf[:, :], 1.0 / 1024.0), w=["ones"])
        E(DVE, lambda: nc.vector.memset(epst[:, :], EPS), w=["epst"])
        E(DVE, lambda: nc.vector.memset(onespad[:, :], 0.0), w=["onespad"])
        E(DVE, lambda: nc.vector.memset(onespad[:, 64:128], 1.0), w=["onespad"])
        E(DVE, lambda: nc.vector.memset(vpad[:, :, :, :], 0.0), w=[("v", i) for i in range(5)])
        E(POOL, lambda: nc.gpsimd.memset(kTlo[:, :, :], 0.0), w=[("klo", g) for g in range(4)])
        E(POOL, lambda: nc.gpsimd.memset(kThi[:, :, :], 0.0), w=[("khi", g) for g in range(4)])
        E(POOL, lambda: nc.gpsimd.memset(halo[:, :, :], 0.0), w=[("halo", c) for c in range(8)])
        E(DVE, lambda: nc.vector.tensor_tensor(out=par[:, P_PBS:P_PBS + 8], in0=par[:, P_PB:P_PB + 8],
                                               in1=par[:, P_PS:P_PS + 8], op=ALU.mult), r=["par"], w=["par2"])
        E(ACT, lambda: nc.scalar.activation(out=par[:, P_SE:P_SE + 8], in_=par[:, P_SK:P_SK + 8], func=AF.Exp),
          r=["par"], w=["par3"])
        PARK = ["par", "par2", "par3"]

        total_pieces = NT * NPP

        def load_piece(i):
            off, ncols, seg = PASS_PIECES[i % NPP]
            s = i % NSLOT
            tr.dma(SP, d_w[s], wslot[s][:, 0:ncols], wbf_d[:, off:off + ncols],
                   reads=[("wbf", seg)], writes=[("w", s)])

        def get_piece():
            return st["use"] % NSLOT

        def done_piece():
            st["use"] += 1
            if st["load"] < total_pieces:
                load_piece(st["load"])
                st["load"] += 1

        def load_x(gb):
            tr.dma(SP, d_in[gb % 4], xin[gb % 4][:, :], x_d[gb * 128:(gb + 1) * 128, :],
                   writes=[("xin", gb % 4)])

        for gb in range(4):
            load_x(gb)
        for i in range(NSLOT):
            load_piece(i)
        st["load"] = NSLOT

        class Norm:
            def __init__(self):
                self.bank = 7
                self.pend = []
                self.n = 0

            def feed(self, c, src=None, key=None):
                k = st["sq"] % 3
                st["sq"] += 1
                src = h[:, c, :] if src is None else src
                key = ("h", c) if key is None else key
                E(ACT, lambda: nc.scalar.activation(out=sqb[k][:, :], in_=src, func=AF.Square),
                  r=[key], w=[("sq", k)])
                self.pend.append(k)

            def mm(self, keep=0):
                while len(self.pend) > keep:
                    k = self.pend.pop(0)
                    n = self.n
                    E(PE, lambda: nc.tensor.matmul(ps[self.bank][:, :], ones_bf[:, :], sqb[k][:, :],
                                                   start=(n == 0), stop=(n == 7)),
                      r=[("sq", k), "ones"], w=[("ps", self.bank)])
                    self.n += 1

            def finish(self):
                self.mm(0)
                assert self.n == 8
                ri = NSCR + (st["rstd"] % 2)
                st["rstd"] += 1
                E(ACT, lambda: nc.scalar.activation(out=SCR[ri][:, 0:512], in_=ps[self.bank][:, :], func=AF.Ln,
                                                    bias=epst[:, 0:1], scale=1.0),
                  r=[("ps", self.bank), "epst"], w=[("scr", ri)])
                E(ACT, lambda: nc.scalar.activation(out=SCR[ri][:, 0:512], in_=SCR[ri][:, 0:512], func=AF.Exp, scale=-0.5),
                  r=[("scr", ri)], w=[("scr", ri)])
                return ri

        def scale_norm(eng, out_ap, wkeys, c, gi, ri, src=None, skey=None):
            src = h[:, c, :] if src is None else src
            skey = ("h", c) if skey is None else skey
            if eng is DVE:
                E(DVE, lambda: nc.vector.scalar_tensor_tensor(out=out_ap, in0=src, scalar=pc(P_G + gi * 8 + c),
                                                              in1=SCR[ri][:, 0:512], op0=ALU.mult, op1=ALU.mult),
                  r=[skey, ("scr", ri)] + PARK + wkeys, w=wkeys)
            else:
                tm = scr()
                E(POOL, lambda: nc.gpsimd.tensor_tensor(out=SCR[tm][:, 0:512], in0=h[:, c, :], in1=SCR[ri][:, 0:512], op=ALU.mult),
                  r=[("h", c), ("scr", ri)], w=[("scr", tm)])
                E(POOL, lambda: nc.gpsimd.tensor_scalar(out=out_ap, in0=SCR[tm][:, 0:512], scalar1=pc(P_G + gi * 8 + c),
                                                        scalar2=None, op0=ALU.mult),
                  r=[("scr", tm)] + PARK + wkeys, w=wkeys)

        def warm(n):
            jb = nb()
            for i in range(n):
                E(PE, lambda: nc.tensor.matmul(ps[jb][:, :], ones_bf[:, :], etab[:, (i % 8) * 512:(i % 8) * 512 + 512],
                                               start=True, stop=True),
                  r=["ones", "etab"], w=[("ps", jb)], sig=(i == n - 1))

        def hn_std(ri, gi):
            warm(26)
            for c in range(8):
                scale_norm(DVE, hn[:, c, :], [("hn", c)], c, gi, ri)

        def ffn(l, t, nm_next, mid=None):
            def halo_load(jj, k):
                for ag in range(2):
                    u = (2 * k + ag) % 4
                    ci = l * 44 + jj * 2 + ag
                    if t > 0:
                        E(POOL, lambda: nc.gpsimd.tensor_copy(UB[u][:, 0:2], carry[:, ci, :]),
                          r=[("cy", ci)], w=[("ub", u, "h")])
                    else:
                        E(POOL, lambda: nc.gpsimd.memset(UB[u][:, 0:2], 0.0), w=[("ub", u, "h")])

            def evac_chunk(ba, bg, jj):
                k = st["ubk"]
                st["ubk"] += 1
                if jj + 1 < FC:
                    halo_load(jj + 1, k + 1)
                info = []
                for ag, bank in ((0, ba), (1, bg)):
                    u = (2 * k + ag) % 4
                    cc = scr()
                    ci = l * 44 + jj * 2 + ag
                    w0 = P_CW + ((l * 3 + 0) * 2 + ag) * 22 + jj
                    w1 = P_CW + ((l * 3 + 1) * 2 + ag) * 22 + jj
                    w2 = P_CW + ((l * 3 + 2) * 2 + ag) * 22 + jj
                    cb = P_CB + (l * 2 + ag) * 22 + jj
                    E(ACT, lambda: nc.scalar.copy(UB[u][:, 2:514], ps[bank][:, :]),
                      r=[("ps", bank)], w=[("ub", u, "m")])
                    E(ACT, lambda: nc.scalar.activation(out=SCR[cc][:, 0:512], in_=ps[bank][:, :], func=AF.Identity,
                                                        bias=pc(cb), scale=pc(w2)),
                      r=[("ps", bank)] + PARK, w=[("scr", cc)])
                    if t < NT - 1:
                        E(POOL, lambda: nc.gpsimd.tensor_copy(carry[:, ci, :], UB[u][:, 512:514]),
                          r=[("ub", u, "m")], w=[("cy", ci)])
                    info.append((u, cc, w0, w1))
                for tap in (1, 0):
                    for (u, cc, w0, w1) in info:
                        wc = w1 if tap == 1 else w0
                        E(DVE, lambda: nc.vector.scalar_tensor_tensor(out=SCR[cc][:, 0:512], in0=UB[u][:, tap:tap + 512], scalar=pc(wc),
                                                                      in1=SCR[cc][:, 0:512], op0=ALU.mult, op1=ALU.add),
                          r=[("ub", u, "m"), ("ub", u, "h"), ("scr", cc)] + PARK, w=[("scr", cc)])
                fin = pend_fin[0]
                pend_fin[0] = (info[0][1], info[1][1], jj)
                if fin is not None:
                    finish_chunk(*fin)

            pend_fin = [None]

            def finish_chunk(sa, sg, jj):
                E(ACT, lambda: nc.scalar.activation(out=SCR[sg][:, 0:512], in_=SCR[sg][:, 0:512], func=AF.Silu),
                  r=[("scr", sg)], w=[("scr", sg)])
                E(POOL, lambda: nc.gpsimd.tensor_tensor(out=act[:, jj, :], in0=SCR[sa][:, 0:512], in1=SCR[sg][:, 0:512],
                                                        op=ALU.mult),
                  r=[("scr", sa), ("scr", sg)], w=[("act", jj)] + ([("hf", jj // 2)] if jj < 16 else []))

            cbs_box = [[]]
            halo_load(0, st["ubk"])
            for j in range(11):
                s = get_piece()
                banks = []
                for sub in range(4):
                    b = nb()
                    banks.append(b)
                    for kc in range(KC):
                        E(PE, lambda: nc.tensor.matmul(ps[b][:, :], wslot[s][:, kc * 512 + sub * 128: kc * 512 + sub * 128 + 128],
                                                       hn[:, kc, :], start=(kc == 0), stop=(kc == KC - 1)),
                          r=[("w", s), ("hn", kc)], w=[("ps", b)], sig=(kc == KC - 1))
                    if sub % 2 == 1:
                        evac_chunk(banks[sub - 1], banks[sub], 2 * j + sub // 2)
                done_piece()
                if j == 5 and mid is not None:
                    cbs_box[0] = mid()
            finish_chunk(*pend_fin[0])
            E(ACT, lambda: nc.scalar.activation(out=epst[:, 4:5], in_=epst[:, 0:1], func=AF.Ln), r=["epst"], w=["epst_d"])
            cbs = cbs_box[0]
            def pb_mm(s, b, kc):
                E(PE, lambda: nc.tensor.matmul(ps[b][:, :], wslot[s][:, kc * 128:(kc + 1) * 128], act[:, kc, :],
                                               start=(kc == 0), stop=(kc == FC - 1)),
                  r=[("w", s), ("act", kc)], w=[("ps", b)], sig=(kc == FC - 1))

            def pb_evac(d, b):
                E(DVE, lambda: nc.vector.tensor_tensor(out=h[:, d, :], in0=ps[b][:, :], in1=h[:, d, :], op=ALU.add),
                  r=[("ps", b), ("h", d)], w=[("h", d)])
                if nm_next is not None:
                    nm_next.feed(d)
                    nm_next.mm(keep=1)
                if d < len(cbs):
                    cbs[d]()

            s0_ = get_piece()
            s1_ = (st["use"] + 1) % NSLOT
            b0_ = nb()
            b1_ = nb()
            for kc in range(FC - 2):
                pb_mm(s0_, b0_, kc)
            for kc in range(FC - 2):
                pb_mm(s1_, b1_, kc)
            for kc in range(FC - 2, FC):
                pb_mm(s0_, b0_, kc)
            for kc in range(FC - 2, FC):
                pb_mm(s1_, b1_, kc)
            done_piece()
            done_piece()
            pb_evac(0, b0_)
            pb_evac(1, b1_)
            for d in range(2, 8):
                s = get_piece()
                b = nb()
                for kc in range(FC):
                    pb_mm(s, b, kc)
                done_piece()
                pb_evac(d, b)
            for cb in cbs[8:]:
                cb()

        def emit_out(t):
            for tb in range(4):
                gb = 4 * t + tb
                xo = gb % 2
                for half in range(2):
                    b = nb()
                    for cc in range(4):
                        c = half * 4 + cc
                        E(PE, lambda: nc.tensor.transpose(ps[b][:, cc * 128:(cc + 1) * 128], hfv[:, c * 512 + tb * 128: c * 512 + (tb + 1) * 128],
                                                          ident[:, :]),
                          r=[("hf", c), "ident"], w=[("ps", b)], sig=(cc == 3))
                    eng = ACT if half == 0 else DVE
                    if eng is ACT:
                        E(ACT, lambda: nc.scalar.copy(xout[xo][:, half * 512:(half + 1) * 512], ps[b][:, :]),
                          r=[("ps", b)], w=[("xout", xo, half)])
                    else:
                        E(DVE, lambda: nc.vector.tensor_copy(xout[xo][:, half * 512:(half + 1) * 512], ps[b][:, :]),
                          r=[("ps", b)], w=[("xout", xo, half)])
                tr.dma(SP, d_out[xo], out_d[gb * 128:(gb + 1) * 128, :], xout[xo][:, :],
                       reads=[("xout", xo, 0), ("xout", xo, 1)])

        STAGES = ["x", "pool", "ffn0", "attn", "ffn1", "final"]
        stop_i = STAGES.index(stop_after)

        def head_s1(t):
            nm = Norm()
            for c in range(8):
                b = nb()
                for tb in range(4):
                    E(PE, lambda: nc.tensor.transpose(ps[b][:, tb * 128:(tb + 1) * 128], xin[tb][:, c * 128:(c + 1) * 128],
                                                      ident[:, :]),
                      r=[("xin", tb), "ident"], w=[("ps", b)], sig=(tb == 3))
                wk = [("hx", c)] + hx_alias(c)
                E(ACT, lambda: nc.scalar.copy(hx(c), ps[b][:, :]), r=[("ps", b)], w=wk)
                nm.feed(c, hx(c), ("hx", c))
                nm.mm(keep=2)
            if t + 1 < NT:
                for tb in range(4):
                    load_x(4 * (t + 1) + tb)
            return nm.finish()

        def head_pool_steps(t, ri):
          for pair in PAIRS:
            g = pair[0] // 2
            L = g + 1
            w = 2 ** L
            hp = {}
            cur = {}
            for pi_, c in enumerate(pair):
                hp[c] = NSCR + 2 + pi_
                E(DVE, lambda: nc.vector.tensor_copy(SCR[hp[c]][:, 0:16], halo[:, c, :]), r=[("halo", c)], w=[("scr", hp[c])])
            yield
            for c in pair:
                scale_norm(DVE, SCR[hp[c]][:, 16:528], [("scr", hp[c])], c, 0, ri, hx(c), ("hx", c))
            yield
            if t < NT - 1:
                for c in pair:
                    E(DVE, lambda: nc.vector.tensor_copy(halo[:, c, :], SCR[hp[c]][:, 512:528]),
                      r=[("scr", hp[c])], w=[("halo", c)])
            for c in pair:
                cur[c] = hp[c]
            s0 = 0
            for lv in range(L):
                sh = 2 ** lv
                s1 = s0 + sh
                for c in pair:
                    nx = scr()
                    cc_ = cur[c]
                    E(DVE, lambda: nc.vector.tensor_tensor(out=SCR[nx][:, s1:528], in0=SCR[cc_][:, s1:528],
                                                           in1=SCR[cc_][:, s1 - sh:528 - sh], op=ALU.add),
                      r=[("scr", cc_)], w=[("scr", nx)])
                    cur[c] = nx
                s0 = s1
                yield
            for c in pair:
                cc_ = cur[c]
                E(DVE, lambda: nc.vector.scalar_tensor_tensor(out=hn[:, c, :], in0=SCR[cc_][:, 16:528], scalar=1.0 / w,
                                                              in1=SCR[hp[c]][:, 16:528], op0=ALU.mult, op1=ALU.subtract),
                  r=[("scr", cc_), ("scr", hp[c])], w=[("hn", c)])
            yield
            if t == 0:
                for c in pair:
                    cc_ = cur[c]
                    tm = scr()
                    E(DVE, lambda: nc.vector.tensor_tensor(out=SCR[tm][:, 0:16], in0=SCR[cc_][:, 16:32],
                                                           in1=par[:, P_IC + g * 16:P_IC + g * 16 + 16], op=ALU.mult),
                      r=[("scr", cc_)] + PARK, w=[("scr", tm)])
                    E(DVE, lambda: nc.vector.tensor_tensor(out=hn[:, c, 0:16], in0=SCR[tm][:, 0:16], in1=SCR[hp[c]][:, 16:32],
                                                           op=ALU.subtract),
                      r=[("scr", tm), ("scr", hp[c]), ("hn", c)], w=[("hn", c)])

        def pool_mm():
            nm2 = Norm()
            for c in range(8):
                g = c // 2
                b = nb()
                for kc in range(2):
                    E(PE, lambda: nc.tensor.matmul(ps[b][:, :], poolw[:, g * 512 + kc * 256 + (c % 2) * 128: g * 512 + kc * 256 + (c % 2) * 128 + 128],
                                                   hn[:, 2 * g + kc, :], start=(kc == 0), stop=(kc == 1)),
                      r=["poolw", ("hn", 2 * g + kc)], w=[("ps", b)], sig=(kc == 1))
                tm = scr()
                E(ACT, lambda: nc.scalar.activation(out=SCR[tm][:, 0:512], in_=ps[b][:, :], func=AF.Identity,
                                                    bias=pc(P_PBS + c), scale=pc(P_PS + c)),
                  r=[("ps", b)] + PARK, w=[("scr", tm)])
                E(DVE, lambda: nc.vector.tensor_tensor(out=h[:, c, :], in0=SCR[tm][:, 0:512], in1=hx(c), op=ALU.add),
                  r=[("scr", tm), ("hx", c)], w=[("h", c)])
            return nm2

        def pool_norm(nm2):
            for c in range(8):
                nm2.feed(c)
                nm2.mm(keep=2)
            return nm2.finish()

        PAIRS = [(7, 6), (5, 4), (3, 2), (1, 0)]

        def make_mid(tn):
            def mid():
                ri_ = head_s1(tn)
                gen = head_pool_steps(tn, ri_)

                def adv(n):
                    def f():
                        for _ in range(n):
                            try:
                                next(gen)
                            except StopIteration:
                                return
                    return f
                return [adv(0)] + [adv(3)] * 7 + [adv(1000)]
            return mid

        assert stop_after == "final"
        ri0 = head_s1(0)
        for _ in head_pool_steps(0, ri0):
            pass
        nm2 = pool_mm()
        ri = pool_norm(nm2)
        hn_std(ri, 1)
        for t in range(NT):
            nm3 = Norm()
            ffn(0, t, nm3)
            if stop_i == 2:
                nm3.mm(0)
                emit_out(t)
                for _ in range(NPP - 19):
                    done_piece()
                continue
            ri = nm3.finish()
            hn_std(ri, 2)

            for i in range(2):
                s = get_piece()
                for cc in range(4):
                    c = i * 4 + cc
                    b = nb()
                    for kc in range(KC):
                        E(PE, lambda: nc.tensor.matmul(ps[b][:, :], wslot[s][:, kc * 512 + cc * 128: kc * 512 + cc * 128 + 128],
                                                       hn[:, kc, :], start=(kc == 0), stop=(kc == KC - 1)),
                          r=[("w", s), ("hn", kc)], w=[("ps", b)], sig=(kc == KC - 1))
                    E(ACT, lambda: nc.scalar.activation(out=qT[:, c, :], in_=ps[b][:, :], func=AF.Identity, bias=pc(P_BQ + c)),
                      r=[("ps", b)] + PARK, w=[("q", c), ("hx", c // 2)])
                done_piece()
            s = get_piece()
            kcol = ((4 * t) % 8) * 128
            for g in range(4):
                b = nb()
                for kc in range(KC):
                    E(PE, lambda: nc.tensor.matmul(ps[b][:, :], wslot[s][:, kc * 512 + g * 128: kc * 512 + g * 128 + 128],
                                                   hn[:, kc, :], start=(kc == 0), stop=(kc == KC - 1)),
                      r=[("w", s), ("hn", kc)], w=[("ps", b)], sig=(kc == KC - 1))
                E(ACT, lambda: nc.scalar.activation(out=kTlo[0:64, g, kcol:kcol + 512], in_=ps[b][0:64, :], func=AF.Identity,
                                                    bias=par[0:64, P_BK + g:P_BK + g + 1]),
                  r=[("ps", b)] + PARK, w=[("klo", g)])
                E(ACT, lambda: nc.scalar.activation(out=kThi[64:128, g, kcol:kcol + 512], in_=ps[b][64:128, :], func=AF.Identity,
                                                    bias=par[64:128, P_BK + g:P_BK + g + 1]),
                  r=[("ps", b)] + PARK, w=[("khi", g)])
            done_piece()
            s = get_piece()
            for tb in range(4):
                if tb % 2 == 0:
                    b = nb()
                off = (tb % 2) * 256
                vs = (4 * t + tb) % 5
                for kc in range(KC):
                    E(PE, lambda: nc.tensor.matmul(ps[b][:, off:off + 256], hn[:, kc, tb * 128:(tb + 1) * 128],
                                                   wslot[s][:, kc * 256:(kc + 1) * 256], start=(kc == 0), stop=(kc == KC - 1)),
                      r=[("w", s), ("hn", kc)], w=[("ps", b)], sig=(kc == KC - 1))
                for g in range(4):
                    E(DVE, lambda: nc.vector.tensor_tensor(out=vpad[:, vs, g, 64:128], in0=ps[b][:, off + g * 64: off + g * 64 + 64],
                                                           in1=par[:, P_BV + g * 64: P_BV + g * 64 + 64], op=ALU.add),
                      r=[("ps", b)] + PARK, w=[("v", vs)])
            done_piece()

            kbs = [-1, 0, 1, 2, 3]

            def scores(c):
                g = c // 2
                pt = PT[c % 2]
                groups = []
                if t > 0:
                    groups.append((0, [(-1, 0, 128, 128), (3, 384, 128, 0)]))
                else:
                    groups.append((0, [(3, 384, 128, 0)]))
                for kb in (0, 1, 2):
                    groups.append((kb + 1, [(kb, kb * 128, 256, 0)]))
                for slot, parts in groups:
                    b = nb()
                    mms = [(kb, qc, n, oc, hh) for (kb, qc, n, oc) in parts for hh in range(2)]
                    for mi, (kb, qc, n, oc, hh) in enumerate(mms):
                        kc0 = ((4 * t + kb) % 8) * 128
                        kt = kTlo if hh == 0 else kThi
                        kk = ("klo", g) if hh == 0 else ("khi", g)
                        E(PE, lambda: nc.tensor.matmul(ps[b][:, hh * 256 + oc: hh * 256 + oc + n], kt[:, g, kc0:kc0 + 128],
                                                       qT[:, c, qc:qc + n], start=True, stop=True),
                          r=[kk, ("q", c)], w=[("ps", b)], sig=(mi == len(mms) - 1))
                    ex = scr()
                    if t > 0 or slot != 0:
                        rngs = [(0, 512)]
                    else:
                        rngs = [(0, 128), (256, 128)]
                    me = DVE if slot % 2 == 1 else POOL
                    for (lo, n) in rngs:
                        E(ACT, lambda: nc.scalar.activation(out=SCR[ex][:, lo:lo + n], in_=ps[b][:, lo:lo + n], func=AF.Exp,
                                                            scale=0.125),
                          r=[("ps", b)], w=[("scr", ex)])
                    for (lo, n) in rngs:
                        E(me, lambda: me.h.tensor_tensor(out=pt[:, slot, lo:lo + n], in0=SCR[ex][:, lo:lo + n],
                                                         in1=etab[:, c * 512 + lo: c * 512 + lo + n], op=ALU.mult),
                          r=[("scr", ex), "etab"], w=[("pt", c % 2, slot)])

            def pv(c):
                g = c // 2
                pt = PT[c % 2]
                bo = nb()
                bd = nb()
                for which in range(2):
                    bk = bo if which == 0 else bd
                    for qb in range(4):
                        terms = []
                        if not (t == 0 and qb == 0):
                            terms.append((qb - 1, 128))
                        terms.append((qb, 0))
                        mms = [(kb, po, hh) for (kb, po) in terms for hh in range(2)]
                        for mi, (kb, po, hh) in enumerate(mms):
                            kbi = 0 if kb in (-1, 3) else kb + 1
                            vs = (4 * t + kb) % 5
                            if which == 0:
                                lhs = vpad[:, vs, g, 64:192] if hh == 0 else vpad[:, vs, g, 0:128]
                                rk = [("v", vs)]
                            else:
                                lhs = onespad[:, 64:192] if hh == 0 else onespad[:, 0:128]
                                rk = ["onespad"]
                            E(PE, lambda: nc.tensor.matmul(ps[bk][:, qb * 128:(qb + 1) * 128], lhs,
                                                           pt[:, kbi, hh * 256 + po: hh * 256 + po + 128],
                                                           start=(mi == 0), stop=(mi == len(mms) - 1)),
                              r=rk + [("pt", c % 2, kbi)], w=[("ps", bk)], sig=(mi == len(mms) - 1))
                rc = scr()
                E(ACT, lambda: nc.scalar.activation(out=SCR[rc][:, 0:512], in_=ps[bd][:, :], func=AF.Ln, bias=pc(P_SE + c), scale=1.0),
                  r=[("ps", bd)] + PARK, w=[("scr", rc)])
                E(ACT, lambda: nc.scalar.activation(out=SCR[rc][:, 0:512], in_=SCR[rc][:, 0:512], func=AF.Exp, scale=-1.0),
                  r=[("scr", rc)], w=[("scr", rc)])
                E(DVE, lambda: nc.vector.tensor_tensor(out=oT[:, c, :], in0=ps[bo][:, :], in1=SCR[rc][:, 0:512], op=ALU.mult),
                  r=[("ps", bo), ("scr", rc)], w=[("o", c), ("hx", 4 + c // 2)])

            scores(0)
            for c in range(8):
                if c + 1 < 8:
                    scores(c + 1)
                pv(c)

            nm4 = Norm()
            for i in range(2):
                s = get_piece()
                for dd in range(4):
                    d = i * 4 + dd
                    b = nb()
                    for kc in range(KC):
                        E(PE, lambda: nc.tensor.matmul(ps[b][:, :], wslot[s][:, kc * 512 + dd * 128: kc * 512 + dd * 128 + 128],
                                                       oT[:, kc, :], start=(kc == 0), stop=(kc == KC - 1)),
                          r=[("w", s), ("o", kc)], w=[("ps", b)], sig=(kc == KC - 1))
                    E(DVE, lambda: nc.vector.tensor_tensor(out=h[:, d, :], in0=ps[b][:, :], in1=h[:, d, :], op=ALU.add),
                      r=[("ps", b), ("h", d)], w=[("h", d)])
                    nm4.feed(d)
                    nm4.mm(keep=1)
                done_piece()
            if stop_i == 3:
                nm4.mm(0)
                emit_out(t)
                for _ in range(19):
                    done_piece()
                continue
            ri = nm4.finish()
            hn_std(ri, 3)

            nm5 = Norm()
            ffn(1, t, nm5, mid=(make_mid(t + 1) if t + 1 < NT else None))
            if stop_i == 4:
                nm5.mm(0)
                emit_out(t)
                continue
            ri = nm5.finish()
            warm(12)
            for c in range(8):
                scale_norm(DVE, hf(c), [("hf", c), ("act", 2 * c), ("act", 2 * c + 1)], c, 4, ri)
            if t + 1 < NT:
                nm2 = pool_mm()
            emit_out(t)
            if t + 1 < NT:
                ri = pool_norm(nm2)
                hn_std(ri, 1)

        for so in d_out:
            if so.cnt:
                SP.h.wait_ge(so.sem, so.cnt)
    return nc


def _wl(W):
    K, N = W.shape
    kc = K // 128
    return np.ascontiguousarray(W.reshape(kc, 128, N).transpose(1, 0, 2).reshape(128, kc * N))


def _fm(v, n):
    return np.ascontiguousarray(np.asarray(v, np.float32).reshape(n, 128).T)


def _prep(pool_w, pool_b, pool_scale, attn_w_qkv, attn_b_qkv, attn_sinks, attn_w_o,
          norm_mix, norm_ffn, ffn_w_in, ffn_conv_w, ffn_conv_b, ffn_w_out, norm_f):
    f32 = np.float32
    cols = []
    pw = np.asarray(pool_w, f32)[0]
    cols.append(np.concatenate([_wl(pw[g]) for g in range(4)], axis=1))
    wqkv = np.asarray(attn_w_qkv, f32)[0]
    wo = np.asarray(attn_w_o, f32)[0]
    for l in range(2):
        if l == 1:
            cols.append(_wl(wqkv[:, 0:512]))
            cols.append(_wl(wqkv[:, 512:1024]))
            sel = np.concatenate([np.tile(1024 + g * 64 + np.arange(64), 2) for g in range(4)])
            cols.append(_wl(wqkv[:, sel]))
            cols.append(_wl(wqkv[:, 1280:1536]))
            cols.append(_wl(wo[:, 0:512]))
            cols.append(_wl(wo[:, 512:1024]))
        win = np.asarray(ffn_w_in, f32)[l]
        for j in range(11):
            sel = np.concatenate([128 * (2 * j) + np.arange(128), FF + 128 * (2 * j) + np.arange(128),
                                  128 * (2 * j + 1) + np.arange(128), FF + 128 * (2 * j + 1) + np.arange(128)])
            cols.append(_wl(win[:, sel]))
        wout = np.asarray(ffn_w_out, f32)[l]
        for d in range(8):
            cols.append(_wl(wout[:, d * 128:(d + 1) * 128]))
    wall = np.ascontiguousarray(np.concatenate(cols, axis=1))
    assert wall.shape == (128, NW), wall.shape

    par = np.zeros((128, NP), f32)
    gains = [np.asarray(norm_mix, f32)[0], np.asarray(norm_ffn, f32)[0], np.asarray(norm_mix, f32)[1],
             np.asarray(norm_ffn, f32)[1], np.asarray(norm_f, f32)]
    for i, gv in enumerate(gains):
        par[:, P_G + i * 8:P_G + i * 8 + 8] = _fm(gv, 8)
    par[:, P_PB:P_PB + 8] = _fm(np.asarray(pool_b, f32)[0].reshape(-1), 8)
    par[:, P_PS:P_PS + 8] = _fm(np.asarray(pool_scale, f32)[0], 8)
    bqkv = np.asarray(attn_b_qkv, f32)[0]
    par[:, P_BQ:P_BQ + 8] = _fm(bqkv[:1024], 8)
    p = np.arange(128)
    for g in range(4):
        par[:, P_BK + g] = bqkv[1024 + g * 64 + (p % 64)]
    sk = np.asarray(attn_sinks, f32)[0]
    for c in range(8):
        par[:, P_SK + c] = sk[2 * c + p // 64]
    cw = np.asarray(ffn_conv_w, f32)
    cb = np.asarray(ffn_conv_b, f32)
    for l in range(2):
        for k in range(3):
            for ag in range(2):
                o = P_CW + ((l * 3 + k) * 2 + ag) * 22
                par[:, o:o + 22] = _fm(cw[l, k, ag * FF:(ag + 1) * FF], 22)
        for ag in range(2):
            o = P_CB + (l * 2 + ag) * 22
            par[:, o:o + 22] = _fm(cb[l, ag * FF:(ag + 1) * FF], 22)
    par[:, P_BV:P_BV + 256] = bqkv[1280:1536][None, :]
    for g in range(4):
        w = 2 ** (g + 1)
        par[:, P_IC + g * 16:P_IC + g * 16 + 16] = (1.0 / np.minimum(np.arange(16) + 1, w).astype(f32))[None, :]
    return wall, par


def _consts():
    f32 = np.float32
    hh = np.arange(1, 17, dtype=f32)
    slopes = np.exp2(f32(-8.0 / 16) * hh).astype(f32)
    j = np.arange(128)[:, None]
    i = np.arange(128)[None, :]
    etab = np.zeros((128, 16, 256), f32)
    for hd in range(16):
        dc = (i - j).astype(f32)
        etab[:, hd, 0:128] = np.where(j <= i, np.exp(-slopes[hd] * np.maximum(dc, 0.0)), 0.0)
        dp = (i + 128 - j).astype(f32)
        etab[:, hd, 128:256] = np.where(j > i, np.exp(-slopes[hd] * dp), 0.0)
    return np.ascontiguousarray(etab.reshape(128, 4096)), np.eye(128, dtype=f32)


_CACHE = {}


def run(inputs, S, stop_after="final", trace=False):
    x = np.asarray(inputs["x"], np.float32)
    B = x.shape[0]
    wall, par = _prep(**{k: v for k, v in inputs.items() if k != "x"})
    etab, ident = _consts()
    key = (S, stop_after)
    if key not in _CACHE:
        _CACHE[key] = build(S, stop_after)
    nc = _CACHE[key]
    in_maps = [{"x": np.ascontiguousarray(x[b, :S]), "wall": wall, "par": par, "etab": etab, "ident": ident}
               for b in range(B)]
    res = run_bass_kernel_spmd(nc, in_maps, core_ids=list(range(B)), trace=trace)
    out = np.stack([np.asarray(r["out"]) for r in res.results]).astype(np.float32)
    return out, res


def kernel(**inputs):
    out, _ = run(inputs, 4096)
    return out
```

```python
import numpy as np
from contextlib import ExitStack
import concourse.bass as bass
import concourse.mybir as mybir
from concourse.bass_utils import run_bass_kernel_spmd

F32 = mybir.dt.float32
BF16 = mybir.dt.bfloat16
AF = mybir.ActivationFunctionType
ALU = mybir.AluOpType

D = 1024
KC = 8
T = 512
FF = 2816
FC = 22
NSLOT = 4
SLOTC = 4096
NSCR = 6
EPS = 1e-6

PASS_PIECES = []
_off = 0
POOLW_OFF = 0
_off += 2048
SEG_BOUNDS = [(0, 2048)]
for _l in range(2):
    if _l == 1:
        _s = _off
        for _n in (4096, 4096, 4096, 2048, 4096, 4096):
            PASS_PIECES.append((_off, _n, len(SEG_BOUNDS)))
            _off += _n
        SEG_BOUNDS.append((_s, _off))
    _s = _off
    for _j in range(11):
        PASS_PIECES.append((_off, 4096, len(SEG_BOUNDS)))
        _off += 4096
    SEG_BOUNDS.append((_s, _off))
    _s = _off
    for _d in range(8):
        PASS_PIECES.append((_off, 2816, len(SEG_BOUNDS)))
        _off += 2816
    SEG_BOUNDS.append((_s, _off))
NW = _off
NPP = len(PASS_PIECES)

P_G = 0
P_PB = 40
P_PS = 48
P_BQ = 56
P_BK = 64
P_SK = 68
P_CW = 76
P_CB = P_CW + 264
P_BV = P_CB + 88
P_IC = P_BV + 256
P_PBS = P_IC + 64
P_SE = P_PBS + 8
NP = P_SE + 8


class SemObj:
    def __init__(self, sem, h=None):
        self.sem = sem
        self.h = h
        self.cnt = 0
        self.waited = {}


class Tracker:
    def __init__(self):
        self.lastw = {}
        self.readers = {}

    def _deps(self, reads, writes):
        deps = {}

        def add(so, v):
            if deps.get(so, 0) < v:
                deps[so] = v
        for k in reads:
            lw = self.lastw.get(k)
            if lw:
                add(*lw)
        for k in writes:
            lw = self.lastw.get(k)
            if lw:
                add(*lw)
            for so, v in self.readers.get(k, {}).items():
                add(so, v)
        return deps

    def _wait(self, eng, deps, skip_self):
        for so, v in deps.items():
            if so is eng and skip_self:
                continue
            if eng.waited.get(so, 0) >= v:
                continue
            eng.h.wait_ge(so.sem, v)
            eng.waited[so] = v

    def _record(self, so, val, reads, writes):
        for k in reads:
            self.readers.setdefault(k, {})[so] = val
        for k in writes:
            self.lastw[k] = (so, val)
            self.readers[k] = {}

    def emit(self, eng, fn, reads=(), writes=(), signal=True, skip_self=False):
        self._wait(eng, self._deps(reads, writes), skip_self)
        ins = fn()
        if signal:
            ins.then_inc(eng.sem, 1)
            eng.cnt += 1
            val = eng.cnt
        else:
            val = eng.cnt + 1
        self._record(eng, val, reads, writes)
        return ins

    def dma(self, q, dsem, out, in_, reads=(), writes=()):
        self._wait(q, self._deps(reads, writes), True)
        q.h.dma_start(out=out, in_=in_).then_inc(dsem.sem, 16)
        dsem.cnt += 16
        self._record(dsem, dsem.cnt, reads, writes)


def build(S, stop_after="final"):
    NT = S // T
    nc = bass.Bass("TRN2", target_bir_lowering=False)
    x_d = nc.dram_tensor("x", [S, D], F32, kind="ExternalInput").ap()
    wall_d = nc.dram_tensor("wall", [128, NW], F32, kind="ExternalInput").ap()
    par_d = nc.dram_tensor("par", [128, NP], F32, kind="ExternalInput").ap()
    etab_d = nc.dram_tensor("etab", [128, 4096], F32, kind="ExternalInput").ap()
    id_d = nc.dram_tensor("ident", [128, 128], F32, kind="ExternalInput").ap()
    out_d = nc.dram_tensor("out", [S, D], F32, kind="ExternalOutput").ap()
    wbf_d = nc.dram_tensor("wbf", [128, NW], BF16).ap()

    with ExitStack() as es:
        def sb(name, shape, dt):
            return es.enter_context(nc.sbuf_tensor(name, shape, dt))

        def sem(name):
            return SemObj(es.enter_context(nc.semaphore(name)))

        def esem(name, h):
            so = sem(name)
            so.h = h
            return so

        PE = esem("pe", nc.tensor)
        ACT = esem("act", nc.scalar)
        DVE = esem("dve", nc.vector)
        POOL = esem("pool", nc.gpsimd)
        SP = esem("sp", nc.sync)
        d_c = [sem(f"dc{i}") for i in range(3)]
        d_seg = [sem(f"dseg{i}") for i in range(len(SEG_BOUNDS))]
        d_w = [sem(f"dw{i}") for i in range(NSLOT)]
        d_pw = sem("dpw")
        d_in = [sem(f"din{i}") for i in range(4)]
        d_out = [sem(f"dout{i}") for i in range(2)]
        tr = Tracker()

        def E(eng, fn, r=(), w=(), sig=True):
            return tr.emit(eng, fn, r, w, sig, skip_self=(eng is PE))

        ident = sb("ident_s", [128, 128], F32)
        par = sb("par_s", [128, NP], F32)
        etab = sb("etab_s", [128, 4096], BF16)
        ones_bf = sb("ones_bf", [128, 128], BF16)
        epst = sb("epst", [128, 8], F32)
        onespad = sb("onespad", [128, 192], BF16)
        poolw = sb("poolw", [128, 2048], BF16)
        wslot = [sb(f"wslot{i}", [128, SLOTC], BF16) for i in range(NSLOT)]
        xin = [sb(f"xin{i}", [128, 1024], F32) for i in range(4)]
        xout = [sb(f"xout{i}", [128, 1024], F32) for i in range(2)]
        h = sb("h", [128, 8, 512], F32)
        hn = sb("hn", [128, 8, 512], BF16)
        sqb = [sb(f"sq{i}", [128, 512], BF16) for i in range(3)]
        actb = sb("actb", [128, FC * 512], BF16)
        hfv = actb[:, 0:8192].bitcast(F32)

        class _A:
            def __getitem__(self, key):
                p, c, cols = key
                return actb[p, c * 512:(c + 1) * 512]
        act = _A()

        def hf(c):
            return hfv[:, c * 512:(c + 1) * 512]
        qo = sb("qo", [128, 8192], BF16)
        hxv = qo[:, :].bitcast(F32)

        class _V:
            def __init__(self, base):
                self.base = base

            def __getitem__(self, key):
                p, c, cols = key
                lo = 0 if cols.start is None else cols.start
                hi = 512 if cols.stop is None else cols.stop
                return qo[p, self.base + c * 512 + lo: self.base + c * 512 + hi]
        qT = _V(0)
        oT = _V(4096)

        def hx(c):
            return hxv[:, c * 512:(c + 1) * 512]

        def hx_alias(c):
            return [("q", 2 * c), ("q", 2 * c + 1)] if c < 4 else [("o", 2 * (c - 4)), ("o", 2 * (c - 4) + 1)]
        kTlo = sb("kTlo", [128, 4, 1024], BF16)
        kThi = sb("kThi", [128, 4, 1024], BF16)
        vpad = sb("vpad", [128, 5, 4, 192], BF16)
        PT = [sb(f"PT{i}", [128, 4, 512], BF16) for i in range(2)]
        SCR = [sb(f"scr{i}", [128, 528], F32) for i in range(NSCR + 4)]
        halo = sb("halo", [128, 8, 16], F32)
        carry = sb("carry", [128, 88, 2], F32)
        UB = [sb(f"ub{i}", [128, 514], F32) for i in range(4)]
        ps = [es.enter_context(nc.psum_tensor(f"ps{i}", [128, 512], F32)) for i in range(8)]

        st = {"bank": 0, "scr": 0, "sq": 0, "use": 0, "load": 0, "rstd": 0, "ubk": 0}

        def nb():
            b = st["bank"] % 7
            st["bank"] += 1
            return b

        def scr():
            i = st["scr"] % NSCR
            st["scr"] += 1
            return i

        def pc(col):
            return par[:, col:col + 1]

        tr.dma(SP, d_c[0], ident[:, :], id_d[:, :], writes=["ident"])
        tr.dma(SP, d_c[1], par[:, :], par_d[:, :], writes=["par"])
        tr.dma(POOL, d_c[2], etab[:, :], etab_d[:, :], writes=["etab"])
        for si, (a, b) in enumerate(SEG_BOUNDS):
            for a2 in range(a, b, 8192):
                b2 = min(b, a2 + 8192)
                tr.dma(POOL, d_seg[si], wbf_d[:, a2:b2], wall_d[:, a2:b2])
            tr.lastw[("wbf", si)] = (d_seg[si], d_seg[si].cnt)
        tr.dma(SP, d_pw, poolw[:, :], wbf_d[:, 0:2048], reads=[("wbf", 0)], writes=["poolw"])

        E(DVE, lambda: nc.vector.memset(ones_bf[:, :], 1.0 / 1024.0), w=["ones"])
        E(DVE, lambda: nc.vector.memset(epst[:, :], EPS), w=["epst"])
        E(DVE, lambda: nc.vector.memset(onespad[:, :], 0.0), w=["onespad"])
        E(DVE, lambda: nc.vector.memset(onespad[:, 64:128], 1.0), w=["onespad"])
        E(DVE, lambda: nc.vector.memset(vpad[:, :, :, :], 0.0), w=[("v", i) for i in range(5)])
        E(POOL, lambda: nc.gpsimd.memset(kTlo[:, :, :], 0.0), w=[("klo", g) for g in range(4)])
        E(POOL, lambda: nc.gpsimd.memset(kThi[:, :, :], 0.0), w=[("khi", g) for g in range(4)])
        E(POOL, lambda: nc.gpsimd.memset(halo[:, :, :], 0.0), w=[("halo", c) for c in range(8)])
        E(DVE, lambda: nc.vector.tensor_tensor(out=par[:, P_PBS:P_PBS + 8], in0=par[:, P_PB:P_PB + 8],
                                               in1=par[:, P_PS:P_PS + 8], op=ALU.mult), r=["par"], w=["par2"])
        E(ACT, lambda: nc.scalar.activation(out=par[:, P_SE:P_SE + 8], in_=par[:, P_SK:P_SK + 8], func=AF.Exp),
          r=["par"], w=["par3"])
        PARK = ["par", "par2", "par3"]

        total_pieces = NT * NPP

        def load_piece(i):
            off, ncols, seg = PASS_PIECES[i % NPP]
            s = i % NSLOT
            tr.dma(SP, d_w[s], wslot[s][:, 0:ncols], wbf_d[:, off:off + ncols],
                   reads=[("wbf", seg)], writes=[("w", s)])

        def get_piece():
            return st["use"] % NSLOT

        def done_piece():
            st["use"] += 1
            if st["load"] < total_pieces:
                load_piece(st["load"])
                st["load"] += 1

        def load_x(gb):
            tr.dma(SP, d_in[gb % 4], xin[gb % 4][:, :], x_d[gb * 128:(gb + 1) * 128, :],
                   writes=[("xin", gb % 4)])

        for gb in range(4):
            load_x(gb)
        for i in range(NSLOT):
            load_piece(i)
        st["load"] = NSLOT

        class Norm:
            def __init__(self):
                self.bank = 7
                self.pend = []
                self.n = 0

            def feed(self, c, src=None, key=None):
                k = st["sq"] % 3
                st["sq"] += 1
                src = h[:, c, :] if src is None else src
                key = ("h", c) if key is None else key
                E(ACT, lambda: nc.scalar.activation(out=sqb[k][:, :], in_=src, func=AF.Square),
                  r=[key], w=[("sq", k)])
                self.pend.append(k)

            def mm(self, keep=0):
                while len(self.pend) > keep:
                    k = self.pend.pop(0)
                    n = self.n
                    E(PE, lambda: nc.tensor.matmul(ps[self.bank][:, :], ones_bf[:, :], sqb[k][:, :],
                                                   start=(n == 0), stop=(n == 7)),
                      r=[("sq", k), "ones"], w=[("ps", self.bank)])
                    self.n += 1

            def finish(self):
                self.mm(0)
                assert self.n == 8
                ri = NSCR + (st["rstd"] % 2)
                st["rstd"] += 1
                E(ACT, lambda: nc.scalar.activation(out=SCR[ri][:, 0:512], in_=ps[self.bank][:, :], func=AF.Ln,
                                                    bias=epst[:, 0:1], scale=1.0),
                  r=[("ps", self.bank), "epst"], w=[("scr", ri)])
                E(ACT, lambda: nc.scalar.activation(out=SCR[ri][:, 0:512], in_=SCR[ri][:, 0:512], func=AF.Exp, scale=-0.5),
                  r=[("scr", ri)], w=[("scr", ri)])
                return ri

        def scale_norm(eng, out_ap, wkeys, c, gi, ri, src=None, skey=None):
            src = h[:, c, :] if src is None else src
            skey = ("h", c) if skey is None else skey
            if eng is DVE:
                E(DVE, lambda: nc.vector.scalar_tensor_tensor(out=out_ap, in0=src, scalar=pc(P_G + gi * 8 + c),
                                                              in1=SCR[ri][:, 0:512], op0=ALU.mult, op1=ALU.mult),
                  r=[skey, ("scr", ri)] + PARK + wkeys, w=wkeys)
            else:
                tm = scr()
                E(POOL, lambda: nc.gpsimd.tensor_tensor(out=SCR[tm][:, 0:512], in0=h[:, c, :], in1=SCR[ri][:, 0:512], op=ALU.mult),
                  r=[("h", c), ("scr", ri)], w=[("scr", tm)])
                E(POOL, lambda: nc.gpsimd.tensor_scalar(out=out_ap, in0=SCR[tm][:, 0:512], scalar1=pc(P_G + gi * 8 + c),
                                                        scalar2=None, op0=ALU.mult),
                  r=[("scr", tm)] + PARK + wkeys, w=wkeys)

        def hn_std(ri, gi):
            jb = nb()
            for i in range(23):
                E(PE, lambda: nc.tensor.matmul(ps[jb][:, :], ones_bf[:, :], etab[:, (i % 8) * 512:(i % 8) * 512 + 512],
                                               start=True, stop=True),
                  r=["ones", "etab"], w=[("ps", jb)], sig=(i == 22))
            for c in range(8):
                scale_norm(DVE, hn[:, c, :], [("hn", c)], c, gi, ri)

        def ffn(l, t, nm_next, mid=None):
            def halo_load(jj, k):
                for ag in range(2):
                    u = (2 * k + ag) % 4
                    ci = l * 44 + jj * 2 + ag
                    if t > 0:
                        E(POOL, lambda: nc.gpsimd.tensor_copy(UB[u][:, 0:2], carry[:, ci, :]),
                          r=[("cy", ci)], w=[("ub", u, "h")])
                    else:
                        E(POOL, lambda: nc.gpsimd.memset(UB[u][:, 0:2], 0.0), w=[("ub", u, "h")])

            def evac_chunk(ba, bg, jj):
                k = st["ubk"]
                st["ubk"] += 1
                if jj + 1 < FC:
                    halo_load(jj + 1, k + 1)
                info = []
                for ag, bank in ((0, ba), (1, bg)):
                    u = (2 * k + ag) % 4
                    cc = scr()
                    ci = l * 44 + jj * 2 + ag
                    w0 = P_CW + ((l * 3 + 0) * 2 + ag) * 22 + jj
                    w1 = P_CW + ((l * 3 + 1) * 2 + ag) * 22 + jj
                    w2 = P_CW + ((l * 3 + 2) * 2 + ag) * 22 + jj
                    cb = P_CB + (l * 2 + ag) * 22 + jj
                    E(ACT, lambda: nc.scalar.copy(UB[u][:, 2:514], ps[bank][:, :]),
                      r=[("ps", bank)], w=[("ub", u, "m")])
                    E(ACT, lambda: nc.scalar.activation(out=SCR[cc][:, 0:512], in_=ps[bank][:, :], func=AF.Identity,
                                                        bias=pc(cb), scale=pc(w2)),
                      r=[("ps", bank)] + PARK, w=[("scr", cc)])
                    if t < NT - 1:
                        E(POOL, lambda: nc.gpsimd.tensor_copy(carry[:, ci, :], UB[u][:, 512:514]),
                          r=[("ub", u, "m")], w=[("cy", ci)])
                    info.append((u, cc, w0, w1))
                for tap in (1, 0):
                    for (u, cc, w0, w1) in info:
                        wc = w1 if tap == 1 else w0
                        E(DVE, lambda: nc.vector.scalar_tensor_tensor(out=SCR[cc][:, 0:512], in0=UB[u][:, tap:tap + 512], scalar=pc(wc),
                                                                      in1=SCR[cc][:, 0:512], op0=ALU.mult, op1=ALU.add),
                          r=[("ub", u, "m"), ("ub", u, "h"), ("scr", cc)] + PARK, w=[("scr", cc)])
                fin = pend_fin[0]
                pend_fin[0] = (info[0][1], info[1][1], jj)
                if fin is not None:
                    finish_chunk(*fin)

            pend_fin = [None]

            def finish_chunk(sa, sg, jj):
                E(ACT, lambda: nc.scalar.activation(out=SCR[sg][:, 0:512], in_=SCR[sg][:, 0:512], func=AF.Silu),
                  r=[("scr", sg)], w=[("scr", sg)])
                E(POOL, lambda: nc.gpsimd.tensor_tensor(out=act[:, jj, :], in0=SCR[sa][:, 0:512], in1=SCR[sg][:, 0:512],
                                                        op=ALU.mult),
                  r=[("scr", sa), ("scr", sg)], w=[("act", jj)] + ([("hf", jj // 2)] if jj < 16 else []))

            cbs_box = [[]]
            halo_load(0, st["ubk"])
            for j in range(11):
                s = get_piece()
                banks = []
                for sub in range(4):
                    b = nb()
                    banks.append(b)
                    for kc in range(KC):
                        E(PE, lambda: nc.tensor.matmul(ps[b][:, :], wslot[s][:, kc * 512 + sub * 128: kc * 512 + sub * 128 + 128],
                                                       hn[:, kc, :], start=(kc == 0), stop=(kc == KC - 1)),
                          r=[("w", s), ("hn", kc)], w=[("ps", b)], sig=(kc == KC - 1))
                    if sub % 2 == 1:
                        evac_chunk(banks[sub - 1], banks[sub], 2 * j + sub // 2)
                done_piece()
                if j == 5 and mid is not None:
                    cbs_box[0] = mid()
            finish_chunk(*pend_fin[0])
            E(ACT, lambda: nc.scalar.activation(out=epst[:, 4:5], in_=epst[:, 0:1], func=AF.Ln), r=["epst"], w=["epst_d"])
            cbs = cbs_box[0]
            def pb_mm(s, b, kc):
                E(PE, lambda: nc.tensor.matmul(ps[b][:, :], wslot[s][:, kc * 128:(kc + 1) * 128], act[:, kc, :],
                                               start=(kc == 0), stop=(kc == FC - 1)),
                  r=[("w", s), ("act", kc)], w=[("ps", b)], sig=(kc == FC - 1))

            def pb_evac(d, b):
                E(DVE, lambda: nc.vector.tensor_tensor(out=h[:, d, :], in0=ps[b][:, :], in1=h[:, d, :], op=ALU.add),
                  r=[("ps", b), ("h", d)], w=[("h", d)])
                if nm_next is not None:
                    nm_next.feed(d)
                    nm_next.mm(keep=1)
                if d < len(cbs):
                    cbs[d]()

            s0_ = get_piece()
            s1_ = (st["use"] + 1) % NSLOT
            b0_ = nb()
            b1_ = nb()
            for kc in range(FC - 2):
                pb_mm(s0_, b0_, kc)
            for kc in range(FC - 2):
                pb_mm(s1_, b1_, kc)
            for kc in range(FC - 2, FC):
                pb_mm(s0_, b0_, kc)
            for kc in range(FC - 2, FC):
                pb_mm(s1_, b1_, kc)
            done_piece()
            done_piece()
            pb_evac(0, b0_)
            pb_evac(1, b1_)
            for d in range(2, 8):
                s = get_piece()
                b = nb()
                for kc in range(FC):
                    pb_mm(s, b, kc)
                done_piece()
                pb_evac(d, b)
            for cb in cbs[8:]:
                cb()

        def emit_out(t):
            for tb in range(4):
                gb = 4 * t + tb
                xo = gb % 2
                for half in range(2):
                    b = nb()
                    for cc in range(4):
                        c = half * 4 + cc
                        E(PE, lambda: nc.tensor.transpose(ps[b][:, cc * 128:(cc + 1) * 128], hfv[:, c * 512 + tb * 128: c * 512 + (tb + 1) * 128],
                                                          ident[:, :]),
                          r=[("hf", c), "ident"], w=[("ps", b)], sig=(cc == 3))
                    eng = ACT if half == 0 else DVE
                    if eng is ACT:
                        E(ACT, lambda: nc.scalar.copy(xout[xo][:, half * 512:(half + 1) * 512], ps[b][:, :]),
                          r=[("ps", b)], w=[("xout", xo, half)])
                    else:
                        E(DVE, lambda: nc.vector.tensor_copy(xout[xo][:, half * 512:(half + 1) * 512], ps[b][:, :]),
                          r=[("ps", b)], w=[("xout", xo, half)])
                tr.dma(SP, d_out[xo], out_d[gb * 128:(gb + 1) * 128, :], xout[xo][:, :],
                       reads=[("xout", xo, 0), ("xout", xo, 1)])

        STAGES = ["x", "pool", "ffn0", "attn", "ffn1", "final"]
        stop_i = STAGES.index(stop_after)

        def head_s1(t):
            nm = Norm()
            for c in range(8):
                b = nb()
                for tb in range(4):
                    E(PE, lambda: nc.tensor.transpose(ps[b][:, tb * 128:(tb + 1) * 128], xin[tb][:, c * 128:(c + 1) * 128],
                                                      ident[:, :]),
                      r=[("xin", tb), "ident"], w=[("ps", b)], sig=(tb == 3))
                wk = [("hx", c)] + hx_alias(c)
                E(ACT, lambda: nc.scalar.copy(hx(c), ps[b][:, :]), r=[("ps", b)], w=wk)
                nm.feed(c, hx(c), ("hx", c))
                nm.mm(keep=2)
            if t + 1 < NT:
                for tb in range(4):
                    load_x(4 * (t + 1) + tb)
            return nm.finish()

        def head_pool_steps(t, ri):
          for pair in PAIRS:
            g = pair[0] // 2
            L = g + 1
            w = 2 ** L
            hp = {}
            cur = {}
            for pi_, c in enumerate(pair):
                hp[c] = NSCR + 2 + pi_
                E(DVE, lambda: nc.vector.tensor_copy(SCR[hp[c]][:, 0:16], halo[:, c, :]), r=[("halo", c)], w=[("scr", hp[c])])
            yield
            for c in pair:
                scale_norm(DVE, SCR[hp[c]][:, 16:528], [("scr", hp[c])], c, 0, ri, hx(c), ("hx", c))
            yield
            if t < NT - 1:
                for c in pair:
                    E(DVE, lambda: nc.vector.tensor_copy(halo[:, c, :], SCR[hp[c]][:, 512:528]),
                      r=[("scr", hp[c])], w=[("halo", c)])
            for c in pair:
                cur[c] = hp[c]
            s0 = 0
            for lv in range(L):
                sh = 2 ** lv
                s1 = s0 + sh
                for c in pair:
                    nx = scr()
                    cc_ = cur[c]
                    E(DVE, lambda: nc.vector.tensor_tensor(out=SCR[nx][:, s1:528], in0=SCR[cc_][:, s1:528],
                                                           in1=SCR[cc_][:, s1 - sh:528 - sh], op=ALU.add),
                      r=[("scr", cc_)], w=[("scr", nx)])
                    cur[c] = nx
                s0 = s1
                yield
            for c in pair:
                cc_ = cur[c]
                E(DVE, lambda: nc.vector.scalar_tensor_tensor(out=hn[:, c, :], in0=SCR[cc_][:, 16:528], scalar=1.0 / w,
                                                              in1=SCR[hp[c]][:, 16:528], op0=ALU.mult, op1=ALU.subtract),
                  r=[("scr", cc_), ("scr", hp[c])], w=[("hn", c)])
            yield
            if t == 0:
                for c in pair:
                    cc_ = cur[c]
                    tm = scr()
                    E(DVE, lambda: nc.vector.tensor_tensor(out=SCR[tm][:, 0:16], in0=SCR[cc_][:, 16:32],
                                                           in1=par[:, P_IC + g * 16:P_IC + g * 16 + 16], op=ALU.mult),
                      r=[("scr", cc_)] + PARK, w=[("scr", tm)])
                    E(DVE, lambda: nc.vector.tensor_tensor(out=hn[:, c, 0:16], in0=SCR[tm][:, 0:16], in1=SCR[hp[c]][:, 16:32],
                                                           op=ALU.subtract),
                      r=[("scr", tm), ("scr", hp[c]), ("hn", c)], w=[("hn", c)])

        def pool_mm():
            nm2 = Norm()
            for c in range(8):
                g = c // 2
                b = nb()
                for kc in range(2):
                    E(PE, lambda: nc.tensor.matmul(ps[b][:, :], poolw[:, g * 512 + kc * 256 + (c % 2) * 128: g * 512 + kc * 256 + (c % 2) * 128 + 128],
                                                   hn[:, 2 * g + kc, :], start=(kc == 0), stop=(kc == 1)),
                      r=["poolw", ("hn", 2 * g + kc)], w=[("ps", b)], sig=(kc == 1))
                tm = scr()
                E(ACT, lambda: nc.scalar.activation(out=SCR[tm][:, 0:512], in_=ps[b][:, :], func=AF.Identity,
                                                    bias=pc(P_PBS + c), scale=pc(P_PS + c)),
                  r=[("ps", b)] + PARK, w=[("scr", tm)])
                E(DVE, lambda: nc.vector.tensor_tensor(out=h[:, c, :], in0=SCR[tm][:, 0:512], in1=hx(c), op=ALU.add),
                  r=[("scr", tm), ("hx", c)], w=[("h", c)])
            return nm2

        def pool_norm(nm2):
            for c in range(8):
                nm2.feed(c)
                nm2.mm(keep=2)
            return nm2.finish()

        PAIRS = [(7, 6), (5, 4), (3, 2), (1, 0)]

        def make_mid(tn):
            def mid():
                ri_ = head_s1(tn)
                gen = head_pool_steps(tn, ri_)

                def adv(n):
                    def f():
                        for _ in range(n):
                            try:
                                next(gen)
                            except StopIteration:
                                return
                    return f
                return [adv(0)] + [adv(3)] * 7 + [adv(1000)]
            return mid

        assert stop_after == "final"
        ri0 = head_s1(0)
        for _ in head_pool_steps(0, ri0):
            pass
        nm2 = pool_mm()
        ri = pool_norm(nm2)
        hn_std(ri, 1)
        for t in range(NT):
            nm3 = Norm()
            ffn(0, t, nm3)
            if stop_i == 2:
                nm3.mm(0)
                emit_out(t)
                for _ in range(NPP - 19):
                    done_piece()
                continue
            ri = nm3.finish()
            hn_std(ri, 2)

            for i in range(2):
                s = get_piece()
                for cc in range(4):
                    c = i * 4 + cc
                    b = nb()
                    for kc in range(KC):
                        E(PE, lambda: nc.tensor.matmul(ps[b][:, :], wslot[s][:, kc * 512 + cc * 128: kc * 512 + cc * 128 + 128],
                                                       hn[:, kc, :], start=(kc == 0), stop=(kc == KC - 1)),
                          r=[("w", s), ("hn", kc)], w=[("ps", b)], sig=(kc == KC - 1))
                    E(ACT, lambda: nc.scalar.activation(out=qT[:, c, :], in_=ps[b][:, :], func=AF.Identity, bias=pc(P_BQ + c)),
                      r=[("ps", b)] + PARK, w=[("q", c), ("hx", c // 2)])
                done_piece()
            s = get_piece()
            kcol = ((4 * t) % 8) * 128
            for g in range(4):
                b = nb()
                for kc in range(KC):
                    E(PE, lambda: nc.tensor.matmul(ps[b][:, :], wslot[s][:, kc * 512 + g * 128: kc * 512 + g * 128 + 128],
                                                   hn[:, kc, :], start=(kc == 0), stop=(kc == KC - 1)),
                      r=[("w", s), ("hn", kc)], w=[("ps", b)], sig=(kc == KC - 1))
                E(ACT, lambda: nc.scalar.activation(out=kTlo[0:64, g, kcol:kcol + 512], in_=ps[b][0:64, :], func=AF.Identity,
                                                    bias=par[0:64, P_BK + g:P_BK + g + 1]),
                  r=[("ps", b)] + PARK, w=[("klo", g)])
                E(ACT, lambda: nc.scalar.activation(out=kThi[64:128, g, kcol:kcol + 512], in_=ps[b][64:128, :], func=AF.Identity,
                                                    bias=par[64:128, P_BK + g:P_BK + g + 1]),
                  r=[("ps", b)] + PARK, w=[("khi", g)])
            done_piece()
            s = get_piece()
            for tb in range(4):
                if tb % 2 == 0:
                    b = nb()
                off = (tb % 2) * 256
                vs = (4 * t + tb) % 5
                for kc in range(KC):
                    E(PE, lambda: nc.tensor.matmul(ps[b][:, off:off + 256], hn[:, kc, tb * 128:(tb + 1) * 128],
                                                   wslot[s][:, kc * 256:(kc + 1) * 256], start=(kc == 0), stop=(kc == KC - 1)),
                      r=[("w", s), ("hn", kc)], w=[("ps", b)], sig=(kc == KC - 1))
                for g in range(4):
                    E(DVE, lambda: nc.vector.tensor_tensor(out=vpad[:, vs, g, 64:128], in0=ps[b][:, off + g * 64: off + g * 64 + 64],
                                                           in1=par[:, P_BV + g * 64: P_BV + g * 64 + 64], op=ALU.add),
                      r=[("ps", b)] + PARK, w=[("v", vs)])
            done_piece()

            kbs = [-1, 0, 1, 2, 3]

            def scores(c):
                g = c // 2
                pt = PT[c % 2]
                groups = []
                if t > 0:
                    groups.append((0, [(-1, 0, 128, 128), (3, 384, 128, 0)]))
                else:
                    groups.append((0, [(3, 384, 128, 0)]))
                for kb in (0, 1, 2):
                    groups.append((kb + 1, [(kb, kb * 128, 256, 0)]))
                for slot, parts in groups:
                    b = nb()
                    mms = [(kb, qc, n, oc, hh) for (kb, qc, n, oc) in parts for hh in range(2)]
                    for mi, (kb, qc, n, oc, hh) in enumerate(mms):
                        kc0 = ((4 * t + kb) % 8) * 128
                        kt = kTlo if hh == 0 else kThi
                        kk = ("klo", g) if hh == 0 else ("khi", g)
                        E(PE, lambda: nc.tensor.matmul(ps[b][:, hh * 256 + oc: hh * 256 + oc + n], kt[:, g, kc0:kc0 + 128],
                                                       qT[:, c, qc:qc + n], start=True, stop=True),
                          r=[kk, ("q", c)], w=[("ps", b)], sig=(mi == len(mms) - 1))
                    ex = scr()
                    if t > 0 or slot != 0:
                        rngs = [(0, 512)]
                    else:
                        rngs = [(0, 128), (256, 128)]
                    me = DVE if slot % 2 == 1 else POOL
                    for (lo, n) in rngs:
                        E(ACT, lambda: nc.scalar.activation(out=SCR[ex][:, lo:lo + n], in_=ps[b][:, lo:lo + n], func=AF.Exp,
                                                            scale=0.125),
                          r=[("ps", b)], w=[("scr", ex)])
                    for (lo, n) in rngs:
                        E(me, lambda: me.h.tensor_tensor(out=pt[:, slot, lo:lo + n], in0=SCR[ex][:, lo:lo + n],
                                                         in1=etab[:, c * 512 + lo: c * 512 + lo + n], op=ALU.mult),
                          r=[("scr", ex), "etab"], w=[("pt", c % 2, slot)])

            def pv(c):
                g = c // 2
                pt = PT[c % 2]
                bo = nb()
                bd = nb()
                for which in range(2):
                    bk = bo if which == 0 else bd
                    for qb in range(4):
                        terms = []
                        if not (t == 0 and qb == 0):
                            terms.append((qb - 1, 128))
                        terms.append((qb, 0))
                        mms = [(kb, po, hh) for (kb, po) in terms for hh in range(2)]
                        for mi, (kb, po, hh) in enumerate(mms):
                            kbi = 0 if kb in (-1, 3) else kb + 1
                            vs = (4 * t + kb) % 5
                            if which == 0:
                                lhs = vpad[:, vs, g, 64:192] if hh == 0 else vpad[:, vs, g, 0:128]
                                rk = [("v", vs)]
                            else:
                                lhs = onespad[:, 64:192] if hh == 0 else onespad[:, 0:128]
                                rk = ["onespad"]
                            E(PE, lambda: nc.tensor.matmul(ps[bk][:, qb * 128:(qb + 1) * 128], lhs,
                                                           pt[:, kbi, hh * 256 + po: hh * 256 + po + 128],
                                                           start=(mi == 0), stop=(mi == len(mms) - 1)),
                              r=rk + [("pt", c % 2, kbi)], w=[("ps", bk)], sig=(mi == len(mms) - 1))
                rc = scr()
                E(ACT, lambda: nc.scalar.activation(out=SCR[rc][:, 0:512], in_=ps[bd][:, :], func=AF.Ln, bias=pc(P_SE + c), scale=1.0),
                  r=[("ps", bd)] + PARK, w=[("scr", rc)])
                E(ACT, lambda: nc.scalar.activation(out=SCR[rc][:, 0:512], in_=SCR[rc][:, 0:512], func=AF.Exp, scale=-1.0),
                  r=[("scr", rc)], w=[("scr", rc)])
                E(DVE, lambda: nc.vector.tensor_tensor(out=oT[:, c, :], in0=ps[bo][:, :], in1=SCR[rc][:, 0:512], op=ALU.mult),
                  r=[("ps", bo), ("scr", rc)], w=[("o", c), ("hx", 4 + c // 2)])

            scores(0)
            for c in range(8):
                if c + 1 < 8:
                    scores(c + 1)
                pv(c)

            nm4 = Norm()
            for i in range(2):
                s = get_piece()
                for dd in range(4):
                    d = i * 4 + dd
                    b = nb()
                    for kc in range(KC):
                        E(PE, lambda: nc.tensor.matmul(ps[b][:, :], wslot[s][:, kc * 512 + dd * 128: kc * 512 + dd * 128 + 128],
                                                       oT[:, kc, :], start=(kc == 0), stop=(kc == KC - 1)),
                          r=[("w", s), ("o", kc)], w=[("ps", b)], sig=(kc == KC - 1))
                    E(DVE, lambda: nc.vector.tensor_tensor(out=h[:, d, :], in0=ps[b][:, :], in1=h[:, d, :], op=ALU.add),
                      r=[("ps", b), ("h", d)], w=[("h", d)])
                    nm4.feed(d)
                    nm4.mm(keep=1)
                done_piece()
            if stop_i == 3:
                nm4.mm(0)
                emit_out(t)
                for _ in range(19):
                    done_piece()
                continue
            ri = nm4.finish()
            hn_std(ri, 3)

            nm5 = Norm()
            ffn(1, t, nm5, mid=(make_mid(t + 1) if t + 1 < NT else None))
            if stop_i == 4:
                nm5.mm(0)
                emit_out(t)
                continue
            ri = nm5.finish()
            for c in range(8):
                scale_norm(DVE, hf(c), [("hf", c), ("act", 2 * c), ("act", 2 * c + 1)], c, 4, ri)
            if t + 1 < NT:
                nm2 = pool_mm()
            emit_out(t)
            if t + 1 < NT:
                ri = pool_norm(nm2)
                hn_std(ri, 1)

        for so in d_out:
            if so.cnt:
                SP.h.wait_ge(so.sem, so.cnt)
    return nc


def _wl(W):
    K, N = W.shape
    kc = K // 128
    return np.ascontiguousarray(W.reshape(kc, 128, N).transpose(1, 0, 2).reshape(128, kc * N))


def _fm(v, n):
    return np.ascontiguousarray(np.asarray(v, np.float32).reshape(n, 128).T)


def _prep(pool_w, pool_b, pool_scale, attn_w_qkv, attn_b_qkv, attn_sinks, attn_w_o,
          norm_mix, norm_ffn, ffn_w_in, ffn_conv_w, ffn_conv_b, ffn_w_out, norm_f):
    f32 = np.float32
    cols = []
    pw = np.asarray(pool_w, f32)[0]
    cols.append(np.concatenate([_wl(pw[g]) for g in range(4)], axis=1))
    wqkv = np.asarray(attn_w_qkv, f32)[0]
    wo = np.asarray(attn_w_o, f32)[0]
    for l in range(2):
        if l == 1:
            cols.append(_wl(wqkv[:, 0:512]))
            cols.append(_wl(wqkv[:, 512:1024]))
            sel = np.concatenate([np.tile(1024 + g * 64 + np.arange(64), 2) for g in range(4)])
            cols.append(_wl(wqkv[:, sel]))
            cols.append(_wl(wqkv[:, 1280:1536]))
            cols.append(_wl(wo[:, 0:512]))
            cols.append(_wl(wo[:, 512:1024]))
        win = np.asarray(ffn_w_in, f32)[l]
        for j in range(11):
            sel = np.concatenate([128 * (2 * j) + np.arange(128), FF + 128 * (2 * j) + np.arange(128),
                                  128 * (2 * j + 1) + np.arange(128), FF + 128 * (2 * j + 1) + np.arange(128)])
            cols.append(_wl(win[:, sel]))
        wout = np.asarray(ffn_w_out, f32)[l]
        for d in range(8):
            cols.append(_wl(wout[:, d * 128:(d + 1) * 128]))
    wall = np.ascontiguousarray(np.concatenate(cols, axis=1))
    assert wall.shape == (128, NW), wall.shape

    par = np.zeros((128, NP), f32)
    gains = [np.asarray(norm_mix, f32)[0], np.asarray(norm_ffn, f32)[0], np.asarray(norm_mix, f32)[1],
             np.asarray(norm_ffn, f32)[1], np.asarray(norm_f, f32)]
    for i, gv in enumerate(gains):
        par[:, P_G + i * 8:P_G + i * 8 + 8] = _fm(gv, 8)
    par[:, P_PB:P_PB + 8] = _fm(np.asarray(pool_b, f32)[0].reshape(-1), 8)
    par[:, P_PS:P_PS + 8] = _fm(np.asarray(pool_scale, f32)[0], 8)
    bqkv = np.asarray(attn_b_qkv, f32)[0]
    par[:, P_BQ:P_BQ + 8] = _fm(bqkv[:1024], 8)
    p = np.arange(128)
    for g in range(4):
        par[:, P_BK + g] = bqkv[1024 + g * 64 + (p % 64)]
    sk = np.asarray(attn_sinks, f32)[0]
    for c in range(8):
        par[:, P_SK + c] = sk[2 * c + p // 64]
    cw = np.asarray(ffn_conv_w, f32)
    cb = np.asarray(ffn_conv_b, f32)
    for l in range(2):
        for k in range(3):
            for ag in range(2):
                o = P_CW + ((l * 3 + k) * 2 + ag) * 22
                par[:, o:o + 22] = _fm(cw[l, k, ag * FF:(ag + 1) * FF], 22)
        for ag in range(2):
            o = P_CB + (l * 2 + ag) * 22
            par[:, o:o + 22] = _fm(cb[l, ag * FF:(ag + 1) * FF], 22)
    par[:, P_BV:P_BV + 256] = bqkv[1280:1536][None, :]
    for g in range(4):
        w = 2 ** (g + 1)
        par[:, P_IC + g * 16:P_IC + g * 16 + 16] = (1.0 / np.minimum(np.arange(16) + 1, w).astype(f32))[None, :]
    return wall, par


def _consts():
    f32 = np.float32
    hh = np.arange(1, 17, dtype=f32)
    slopes = np.exp2(f32(-8.0 / 16) * hh).astype(f32)
    j = np.arange(128)[:, None]
    i = np.arange(128)[None, :]
    etab = np.zeros((128, 16, 256), f32)
    for hd in range(16):
        dc = (i - j).astype(f32)
        etab[:, hd, 0:128] = np.where(j <= i, np.exp(-slopes[hd] * np.maximum(dc, 0.0)), 0.0)
        dp = (i + 128 - j).astype(f32)
        etab[:, hd, 128:256] = np.where(j > i, np.exp(-slopes[hd] * dp), 0.0)
    return np.ascontiguousarray(etab.reshape(128, 4096)), np.eye(128, dtype=f32)


_CACHE = {}


def run(inputs, S, stop_after="final", trace=False):
    x = np.asarray(inputs["x"], np.float32)
    B = x.shape[0]
    wall, par = _prep(**{k: v for k, v in inputs.items() if k != "x"})
    etab, ident = _consts()
    key = (S, stop_after)
    if key not in _CACHE:
        _CACHE[key] = build(S, stop_after)
    nc = _CACHE[key]
    in_maps = [{"x": np.ascontiguousarray(x[b, :S]), "wall": wall, "par": par, "etab": etab, "ident": ident}
               for b in range(B)]
    res = run_bass_kernel_spmd(nc, in_maps, core_ids=list(range(B)), trace=trace)
    out = np.stack([np.asarray(r["out"]) for r in res.results]).astype(np.float32)
    return out, res


def kernel(**inputs):
    out, _ = run(inputs, 4096)
    return out
```

```python
import numpy as np
from contextlib import ExitStack
import concourse.bass as bass
import concourse.mybir as mybir
from concourse.bass_utils import run_bass_kernel_spmd

F32 = mybir.dt.float32
BF16 = mybir.dt.bfloat16
AF = mybir.ActivationFunctionType
ALU = mybir.AluOpType

D = 1024
KC = 8
T = 512
FF = 2816
FC = 22
NSLOT = 4
SLOTC = 4096
NSCR = 6
EPS = 1e-6

PASS_PIECES = []
_off = 0
POOLW_OFF = 0
_off += 2048
SEG_BOUNDS = [(0, 2048)]
for _l in range(2):
    if _l == 1:
        _s = _off
        for _n in (4096, 4096, 4096, 2048, 4096, 4096):
            PASS_PIECES.append((_off, _n, len(SEG_BOUNDS)))
            _off += _n
        SEG_BOUNDS.append((_s, _off))
    _s = _off
    for _j in range(11):
        PASS_PIECES.append((_off, 4096, len(SEG_BOUNDS)))
        _off += 4096
    SEG_BOUNDS.append((_s, _off))
    _s = _off
    for _d in range(8):
        PASS_PIECES.append((_off, 2816, len(SEG_BOUNDS)))
        _off += 2816
    SEG_BOUNDS.append((_s, _off))
NW = _off
NPP = len(PASS_PIECES)

P_G = 0
P_PB = 40
P_PS = 48
P_BQ = 56
P_BK = 64
P_SK = 68
P_CW = 76
P_CB = P_CW + 264
P_BV = P_CB + 88
P_IC = P_BV + 256
P_PBS = P_IC + 64
P_SE = P_PBS + 8
NP = P_SE + 8


class SemObj:
    def __init__(self, sem, h=None):
        self.sem = sem
        self.h = h
        self.cnt = 0
        self.waited = {}


class Tracker:
    def __init__(self):
        self.lastw = {}
        self.readers = {}

    def _deps(self, reads, writes):
        deps = {}

        def add(so, v):
            if deps.get(so, 0) < v:
                deps[so] = v
        for k in reads:
            lw = self.lastw.get(k)
            if lw:
                add(*lw)
        for k in writes:
            lw = self.lastw.get(k)
            if lw:
                add(*lw)
            for so, v in self.readers.get(k, {}).items():
                add(so, v)
        return deps

    def _wait(self, eng, deps, skip_self):
        for so, v in deps.items():
            if so is eng and skip_self:
                continue
            if eng.waited.get(so, 0) >= v:
                continue
            eng.h.wait_ge(so.sem, v)
            eng.waited[so] = v

    def _record(self, so, val, reads, writes):
        for k in reads:
            self.readers.setdefault(k, {})[so] = val
        for k in writes:
            self.lastw[k] = (so, val)
            self.readers[k] = {}

    def emit(self, eng, fn, reads=(), writes=(), signal=True, skip_self=False):
        self._wait(eng, self._deps(reads, writes), skip_self)
        ins = fn()
        if signal:
            ins.then_inc(eng.sem, 1)
            eng.cnt += 1
            val = eng.cnt
        else:
            val = eng.cnt + 1
        self._record(eng, val, reads, writes)
        return ins

    def dma(self, q, dsem, out, in_, reads=(), writes=()):
        self._wait(q, self._deps(reads, writes), True)
        q.h.dma_start(out=out, in_=in_).then_inc(dsem.sem, 16)
        dsem.cnt += 16
        self._record(dsem, dsem.cnt, reads, writes)


def build(S, stop_after="final"):
    NT = S // T
    nc = bass.Bass("TRN2", target_bir_lowering=False)
    x_d = nc.dram_tensor("x", [S, D], F32, kind="ExternalInput").ap()
    wall_d = nc.dram_tensor("wall", [128, NW], F32, kind="ExternalInput").ap()
    par_d = nc.dram_tensor("par", [128, NP], F32, kind="ExternalInput").ap()
    etab_d = nc.dram_tensor("etab", [128, 4096], F32, kind="ExternalInput").ap()
    id_d = nc.dram_tensor("ident", [128, 128], F32, kind="ExternalInput").ap()
    out_d = nc.dram_tensor("out", [S, D], F32, kind="ExternalOutput").ap()
    wbf_d = nc.dram_tensor("wbf", [128, NW], BF16).ap()

    with ExitStack() as es:
        def sb(name, shape, dt):
            return es.enter_context(nc.sbuf_tensor(name, shape, dt))

        def sem(name):
            return SemObj(es.enter_context(nc.semaphore(name)))

        def esem(name, h):
            so = sem(name)
            so.h = h
            return so

        PE = esem("pe", nc.tensor)
        ACT = esem("act", nc.scalar)
        DVE = esem("dve", nc.vector)
        POOL = esem("pool", nc.gpsimd)
        SP = esem("sp", nc.sync)
        d_c = [sem(f"dc{i}") for i in range(3)]
        d_seg = [sem(f"dseg{i}") for i in range(len(SEG_BOUNDS))]
        d_w = [sem(f"dw{i}") for i in range(NSLOT)]
        d_pw = sem("dpw")
        d_in = [sem(f"din{i}") for i in range(4)]
        d_out = [sem(f"dout{i}") for i in range(2)]
        tr = Tracker()

        def E(eng, fn, r=(), w=(), sig=True):
            return tr.emit(eng, fn, r, w, sig, skip_self=(eng is PE))

        ident = sb("ident_s", [128, 128], F32)
        par = sb("par_s", [128, NP], F32)
        etab = sb("etab_s", [128, 4096], BF16)
        ones_bf = sb("ones_bf", [128, 128], BF16)
        epst = sb("epst", [128, 8], F32)
        onespad = sb("onespad", [128, 192], BF16)
        poolw = sb("poolw", [128, 2048], BF16)
        wslot = [sb(f"wslot{i}", [128, SLOTC], BF16) for i in range(NSLOT)]
        xin = [sb(f"xin{i}", [128, 1024], F32) for i in range(4)]
        xout = [sb(f"xout{i}", [128, 1024], F32) for i in range(2)]
        h = sb("h", [128, 8, 512], F32)
        hn = sb("hn", [128, 8, 512], BF16)
        sqb = [sb(f"sq{i}", [128, 512], BF16) for i in range(3)]
        actb = sb("actb", [128, FC * 512], BF16)
        hfv = actb[:, 0:8192].bitcast(F32)

        class _A:
            def __getitem__(self, key):
                p, c, cols = key
                return actb[p, c * 512:(c + 1) * 512]
        act = _A()

        def hf(c):
            return hfv[:, c * 512:(c + 1) * 512]
        qo = sb("qo", [128, 8192], BF16)
        hxv = qo[:, :].bitcast(F32)

        class _V:
            def __init__(self, base):
                self.base = base

            def __getitem__(self, key):
                p, c, cols = key
                lo = 0 if cols.start is None else cols.start
                hi = 512 if cols.stop is None else cols.stop
                return qo[p, self.base + c * 512 + lo: self.base + c * 512 + hi]
        qT = _V(0)
        oT = _V(4096)

        def hx(c):
            return hxv[:, c * 512:(c + 1) * 512]

        def hx_alias(c):
            return [("q", 2 * c), ("q", 2 * c + 1)] if c < 4 else [("o", 2 * (c - 4)), ("o", 2 * (c - 4) + 1)]
        kTlo = sb("kTlo", [128, 4, 1024], BF16)
        kThi = sb("kThi", [128, 4, 1024], BF16)
        vpad = sb("vpad", [128, 5, 4, 192], BF16)
        PT = [sb(f"PT{i}", [128, 4, 512], BF16) for i in range(2)]
        SCR = [sb(f"scr{i}", [128, 528], F32) for i in range(NSCR + 4)]
        halo = sb("halo", [128, 8, 16], F32)
        carry = sb("carry", [128, 88, 2], F32)
        UB = [sb(f"ub{i}", [128, 514], F32) for i in range(4)]
        ps = [es.enter_context(nc.psum_tensor(f"ps{i}", [128, 512], F32)) for i in range(8)]

        st = {"bank": 0, "scr": 0, "sq": 0, "use": 0, "load": 0, "rstd": 0, "ubk": 0}

        def nb():
            b = st["bank"] % 7
            st["bank"] += 1
            return b

        def scr():
            i = st["scr"] % NSCR
            st["scr"] += 1
            return i

        def pc(col):
            return par[:, col:col + 1]

        tr.dma(SP, d_c[0], ident[:, :], id_d[:, :], writes=["ident"])
        tr.dma(SP, d_c[1], par[:, :], par_d[:, :], writes=["par"])
        tr.dma(POOL, d_c[2], etab[:, :], etab_d[:, :], writes=["etab"])
        for si, (a, b) in enumerate(SEG_BOUNDS):
            for a2 in range(a, b, 8192):
                b2 = min(b, a2 + 8192)
                tr.dma(POOL, d_seg[si], wbf_d[:, a2:b2], wall_d[:, a2:b2])
            tr.lastw[("wbf", si)] = (d_seg[si], d_seg[si].cnt)
        tr.dma(SP, d_pw, poolw[:, :], wbf_d[:, 0:2048], reads=[("wbf", 0)], writes=["poolw"])

        E(DVE, lambda: nc.vector.memset(ones_bf[:, :], 1.0 / 1024.0), w=["ones"])
        E(DVE, lambda: nc.vector.memset(epst[:, :], EPS), w=["epst"])
        E(DVE, lambda: nc.vector.memset(onespad[:, :], 0.0), w=["onespad"])
        E(DVE, lambda: nc.vector.memset(onespad[:, 64:128], 1.0), w=["onespad"])
        E(DVE, lambda: nc.vector.memset(vpad[:, :, :, :], 0.0), w=[("v", i) for i in range(5)])
        E(POOL, lambda: nc.gpsimd.memset(kTlo[:, :, :], 0.0), w=[("klo", g) for g in range(4)])
        E(POOL, lambda: nc.gpsimd.memset(kThi[:, :, :], 0.0), w=[("khi", g) for g in range(4)])
        E(POOL, lambda: nc.gpsimd.memset(halo[:, :, :], 0.0), w=[("halo", c) for c in range(8)])
        E(DVE, lambda: nc.vector.tensor_tensor(out=par[:, P_PBS:P_PBS + 8], in0=par[:, P_PB:P_PB + 8],
                                               in1=par[:, P_PS:P_PS + 8], op=ALU.mult), r=["par"], w=["par2"])
        E(ACT, lambda: nc.scalar.activation(out=par[:, P_SE:P_SE + 8], in_=par[:, P_SK:P_SK + 8], func=AF.Exp),
          r=["par"], w=["par3"])
        PARK = ["par", "par2", "par3"]

        total_pieces = NT * NPP

        def load_piece(i):
            off, ncols, seg = PASS_PIECES[i % NPP]
            s = i % NSLOT
            tr.dma(SP, d_w[s], wslot[s][:, 0:ncols], wbf_d[:, off:off + ncols],
                   reads=[("wbf", seg)], writes=[("w", s)])

        def get_piece():
            return st["use"] % NSLOT

        def done_piece():
            st["use"] += 1
            if st["load"] < total_pieces:
                load_piece(st["load"])
                st["load"] += 1

        def load_x(gb):
            tr.dma(SP, d_in[gb % 4], xin[gb % 4][:, :], x_d[gb * 128:(gb + 1) * 128, :],
                   writes=[("xin", gb % 4)])

        for gb in range(4):
            load_x(gb)
        for i in range(NSLOT):
            load_piece(i)
        st["load"] = NSLOT

        class Norm:
            def __init__(self):
                self.bank = 7
                self.pend = []
                self.n = 0

            def feed(self, c, src=None, key=None):
                k = st["sq"] % 3
                st["sq"] += 1
                src = h[:, c, :] if src is None else src
                key = ("h", c) if key is None else key
                E(ACT, lambda: nc.scalar.activation(out=sqb[k][:, :], in_=src, func=AF.Square),
                  r=[key], w=[("sq", k)])
                self.pend.append(k)

            def mm(self, keep=0):
                while len(self.pend) > keep:
                    k = self.pend.pop(0)
                    n = self.n
                    E(PE, lambda: nc.tensor.matmul(ps[self.bank][:, :], ones_bf[:, :], sqb[k][:, :],
                                                   start=(n == 0), stop=(n == 7)),
                      r=[("sq", k), "ones"], w=[("ps", self.bank)])
                    self.n += 1

            def finish(self):
                self.mm(0)
                assert self.n == 8
                ri = NSCR + (st["rstd"] % 2)
                st["rstd"] += 1
                E(ACT, lambda: nc.scalar.activation(out=SCR[ri][:, 0:512], in_=ps[self.bank][:, :], func=AF.Ln,
                                                    bias=epst[:, 0:1], scale=1.0),
                  r=[("ps", self.bank), "epst"], w=[("scr", ri)])
                E(ACT, lambda: nc.scalar.activation(out=SCR[ri][:, 0:512], in_=SCR[ri][:, 0:512], func=AF.Exp, scale=-0.5),
                  r=[("scr", ri)], w=[("scr", ri)])
                return ri

        def scale_norm(eng, out_ap, wkeys, c, gi, ri, src=None, skey=None):
            src = h[:, c, :] if src is None else src
            skey = ("h", c) if skey is None else skey
            if eng is DVE:
                E(DVE, lambda: nc.vector.scalar_tensor_tensor(out=out_ap, in0=src, scalar=pc(P_G + gi * 8 + c),
                                                              in1=SCR[ri][:, 0:512], op0=ALU.mult, op1=ALU.mult),
                  r=[skey, ("scr", ri)] + PARK + wkeys, w=wkeys)
            else:
                tm = scr()
                E(POOL, lambda: nc.gpsimd.tensor_tensor(out=SCR[tm][:, 0:512], in0=h[:, c, :], in1=SCR[ri][:, 0:512], op=ALU.mult),
                  r=[("h", c), ("scr", ri)], w=[("scr", tm)])
                E(POOL, lambda: nc.gpsimd.tensor_scalar(out=out_ap, in0=SCR[tm][:, 0:512], scalar1=pc(P_G + gi * 8 + c),
                                                        scalar2=None, op0=ALU.mult),
                  r=[("scr", tm)] + PARK + wkeys, w=wkeys)

        def hn_std(ri, gi):
            jb = nb()
            for i in range(18):
                E(PE, lambda: nc.tensor.matmul(ps[jb][:, :], ones_bf[:, :], etab[:, (i % 8) * 512:(i % 8) * 512 + 512],
                                               start=True, stop=True),
                  r=["ones", "etab"], w=[("ps", jb)], sig=(i == 17))
            for c in range(8):
                scale_norm(DVE, hn[:, c, :], [("hn", c)], c, gi, ri)

        def ffn(l, t, nm_next, mid=None):
            def halo_load(jj, k):
                for ag in range(2):
                    u = (2 * k + ag) % 4
                    ci = l * 44 + jj * 2 + ag
                    if t > 0:
                        E(POOL, lambda: nc.gpsimd.tensor_copy(UB[u][:, 0:2], carry[:, ci, :]),
                          r=[("cy", ci)], w=[("ub", u, "h")])
                    else:
                        E(POOL, lambda: nc.gpsimd.memset(UB[u][:, 0:2], 0.0), w=[("ub", u, "h")])

            def evac_chunk(ba, bg, jj):
                k = st["ubk"]
                st["ubk"] += 1
                if jj + 1 < FC:
                    halo_load(jj + 1, k + 1)
                info = []
                for ag, bank in ((0, ba), (1, bg)):
                    u = (2 * k + ag) % 4
                    cc = scr()
                    ci = l * 44 + jj * 2 + ag
                    w0 = P_CW + ((l * 3 + 0) * 2 + ag) * 22 + jj
                    w1 = P_CW + ((l * 3 + 1) * 2 + ag) * 22 + jj
                    w2 = P_CW + ((l * 3 + 2) * 2 + ag) * 22 + jj
                    cb = P_CB + (l * 2 + ag) * 22 + jj
                    E(ACT, lambda: nc.scalar.copy(UB[u][:, 2:514], ps[bank][:, :]),
                      r=[("ps", bank)], w=[("ub", u, "m")])
                    E(ACT, lambda: nc.scalar.activation(out=SCR[cc][:, 0:512], in_=ps[bank][:, :], func=AF.Identity,
                                                        bias=pc(cb), scale=pc(w2)),
                      r=[("ps", bank)] + PARK, w=[("scr", cc)])
                    if t < NT - 1:
                        E(POOL, lambda: nc.gpsimd.tensor_copy(carry[:, ci, :], UB[u][:, 512:514]),
                          r=[("ub", u, "m")], w=[("cy", ci)])
                    info.append((u, cc, w0, w1))
                for tap in (1, 0):
                    for (u, cc, w0, w1) in info:
                        wc = w1 if tap == 1 else w0
                        E(DVE, lambda: nc.vector.scalar_tensor_tensor(out=SCR[cc][:, 0:512], in0=UB[u][:, tap:tap + 512], scalar=pc(wc),
                                                                      in1=SCR[cc][:, 0:512], op0=ALU.mult, op1=ALU.add),
                          r=[("ub", u, "m"), ("ub", u, "h"), ("scr", cc)] + PARK, w=[("scr", cc)])
                fin = pend_fin[0]
                pend_fin[0] = (info[0][1], info[1][1], jj)
                if fin is not None:
                    finish_chunk(*fin)

            pend_fin = [None]

            def finish_chunk(sa, sg, jj):
                E(ACT, lambda: nc.scalar.activation(out=SCR[sg][:, 0:512], in_=SCR[sg][:, 0:512], func=AF.Silu),
                  r=[("scr", sg)], w=[("scr", sg)])
                E(POOL, lambda: nc.gpsimd.tensor_tensor(out=act[:, jj, :], in0=SCR[sa][:, 0:512], in1=SCR[sg][:, 0:512],
                                                        op=ALU.mult),
                  r=[("scr", sa), ("scr", sg)], w=[("act", jj)] + ([("hf", jj // 2)] if jj < 16 else []))

            cbs_box = [[]]
            halo_load(0, st["ubk"])
            for j in range(11):
                s = get_piece()
                banks = []
                for sub in range(4):
                    b = nb()
                    banks.append(b)
                    for kc in range(KC):
                        E(PE, lambda: nc.tensor.matmul(ps[b][:, :], wslot[s][:, kc * 512 + sub * 128: kc * 512 + sub * 128 + 128],
                                                       hn[:, kc, :], start=(kc == 0), stop=(kc == KC - 1)),
                          r=[("w", s), ("hn", kc)], w=[("ps", b)], sig=(kc == KC - 1))
                    if sub % 2 == 1:
                        evac_chunk(banks[sub - 1], banks[sub], 2 * j + sub // 2)
                done_piece()
                if j == 5 and mid is not None:
                    cbs_box[0] = mid()
            finish_chunk(*pend_fin[0])
            E(ACT, lambda: nc.scalar.activation(out=epst[:, 4:5], in_=epst[:, 0:1], func=AF.Ln), r=["epst"], w=["epst_d"])
            cbs = cbs_box[0]
            def pb_mm(s, b, kc):
                E(PE, lambda: nc.tensor.matmul(ps[b][:, :], wslot[s][:, kc * 128:(kc + 1) * 128], act[:, kc, :],
                                               start=(kc == 0), stop=(kc == FC - 1)),
                  r=[("w", s), ("act", kc)], w=[("ps", b)], sig=(kc == FC - 1))

            def pb_evac(d, b):
                E(DVE, lambda: nc.vector.tensor_tensor(out=h[:, d, :], in0=ps[b][:, :], in1=h[:, d, :], op=ALU.add),
                  r=[("ps", b), ("h", d)], w=[("h", d)])
                if nm_next is not None:
                    nm_next.feed(d)
                    nm_next.mm(keep=1)
                if d < len(cbs):
                    cbs[d]()

            s0_ = get_piece()
            s1_ = (st["use"] + 1) % NSLOT
            b0_ = nb()
            b1_ = nb()
            for kc in range(FC - 2):
                pb_mm(s0_, b0_, kc)
            for kc in range(FC - 2):
                pb_mm(s1_, b1_, kc)
            for kc in range(FC - 2, FC):
                pb_mm(s0_, b0_, kc)
            for kc in range(FC - 2, FC):
                pb_mm(s1_, b1_, kc)
            done_piece()
            done_piece()
            pb_evac(0, b0_)
            pb_evac(1, b1_)
            for d in range(2, 8):
                s = get_piece()
                b = nb()
                for kc in range(FC):
                    pb_mm(s, b, kc)
                done_piece()
                pb_evac(d, b)
            for cb in cbs[8:]:
                cb()

        def emit_out(t):
            for tb in range(4):
                gb = 4 * t + tb
                xo = gb % 2
                for half in range(2):
                    b = nb()
                    for cc in range(4):
                        c = half * 4 + cc
                        E(PE, lambda: nc.tensor.transpose(ps[b][:, cc * 128:(cc + 1) * 128], hfv[:, c * 512 + tb * 128: c * 512 + (tb + 1) * 128],
                                                          ident[:, :]),
                          r=[("hf", c), "ident"], w=[("ps", b)], sig=(cc == 3))
                    eng = ACT if half == 0 else DVE
                    if eng is ACT:
                        E(ACT, lambda: nc.scalar.copy(xout[xo][:, half * 512:(half + 1) * 512], ps[b][:, :]),
                          r=[("ps", b)], w=[("xout", xo, half)])
                    else:
                        E(DVE, lambda: nc.vector.tensor_copy(xout[xo][:, half * 512:(half + 1) * 512], ps[b][:, :]),
                          r=[("ps", b)], w=[("xout", xo, half)])
                tr.dma(SP, d_out[xo], out_d[gb * 128:(gb + 1) * 128, :], xout[xo][:, :],
                       reads=[("xout", xo, 0), ("xout", xo, 1)])

        STAGES = ["x", "pool", "ffn0", "attn", "ffn1", "final"]
        stop_i = STAGES.index(stop_after)

        def head_s1(t):
            nm = Norm()
            for c in range(8):
                b = nb()
                for tb in range(4):
                    E(PE, lambda: nc.tensor.transpose(ps[b][:, tb * 128:(tb + 1) * 128], xin[tb][:, c * 128:(c + 1) * 128],
                                                      ident[:, :]),
                      r=[("xin", tb), "ident"], w=[("ps", b)], sig=(tb == 3))
                wk = [("hx", c)] + hx_alias(c)
                E(ACT, lambda: nc.scalar.copy(hx(c), ps[b][:, :]), r=[("ps", b)], w=wk)
                nm.feed(c, hx(c), ("hx", c))
                nm.mm(keep=2)
            if t + 1 < NT:
                for tb in range(4):
                    load_x(4 * (t + 1) + tb)
            return nm.finish()

        def head_pool_steps(t, ri):
          for pair in PAIRS:
            g = pair[0] // 2
            L = g + 1
            w = 2 ** L
            hp = {}
            cur = {}
            for pi_, c in enumerate(pair):
                hp[c] = NSCR + 2 + pi_
                E(DVE, lambda: nc.vector.tensor_copy(SCR[hp[c]][:, 0:16], halo[:, c, :]), r=[("halo", c)], w=[("scr", hp[c])])
            yield
            for c in pair:
                scale_norm(DVE, SCR[hp[c]][:, 16:528], [("scr", hp[c])], c, 0, ri, hx(c), ("hx", c))
            yield
            if t < NT - 1:
                for c in pair:
                    E(DVE, lambda: nc.vector.tensor_copy(halo[:, c, :], SCR[hp[c]][:, 512:528]),
                      r=[("scr", hp[c])], w=[("halo", c)])
            for c in pair:
                cur[c] = hp[c]
            s0 = 0
            for lv in range(L):
                sh = 2 ** lv
                s1 = s0 + sh
                for c in pair:
                    nx = scr()
                    cc_ = cur[c]
                    E(DVE, lambda: nc.vector.tensor_tensor(out=SCR[nx][:, s1:528], in0=SCR[cc_][:, s1:528],
                                                           in1=SCR[cc_][:, s1 - sh:528 - sh], op=ALU.add),
                      r=[("scr", cc_)], w=[("scr", nx)])
                    cur[c] = nx
                s0 = s1
                yield
            for c in pair:
                cc_ = cur[c]
                E(DVE, lambda: nc.vector.scalar_tensor_tensor(out=hn[:, c, :], in0=SCR[cc_][:, 16:528], scalar=1.0 / w,
                                                              in1=SCR[hp[c]][:, 16:528], op0=ALU.mult, op1=ALU.subtract),
                  r=[("scr", cc_), ("scr", hp[c])], w=[("hn", c)])
            yield
            if t == 0:
                for c in pair:
                    cc_ = cur[c]
                    tm = scr()
                    E(DVE, lambda: nc.vector.tensor_tensor(out=SCR[tm][:, 0:16], in0=SCR[cc_][:, 16:32],
                                                           in1=par[:, P_IC + g * 16:P_IC + g * 16 + 16], op=ALU.mult),
                      r=[("scr", cc_)] + PARK, w=[("scr", tm)])
                    E(DVE, lambda: nc.vector.tensor_tensor(out=hn[:, c, 0:16], in0=SCR[tm][:, 0:16], in1=SCR[hp[c]][:, 16:32],
                                                           op=ALU.subtract),
                      r=[("scr", tm), ("scr", hp[c]), ("hn", c)], w=[("hn", c)])

        def pool_mm():
            nm2 = Norm()
            for c in range(8):
                g = c // 2
                b = nb()
                for kc in range(2):
                    E(PE, lambda: nc.tensor.matmul(ps[b][:, :], poolw[:, g * 512 + kc * 256 + (c % 2) * 128: g * 512 + kc * 256 + (c % 2) * 128 + 128],
                                                   hn[:, 2 * g + kc, :], start=(kc == 0), stop=(kc == 1)),
                      r=["poolw", ("hn", 2 * g + kc)], w=[("ps", b)], sig=(kc == 1))
                tm = scr()
                E(ACT, lambda: nc.scalar.activation(out=SCR[tm][:, 0:512], in_=ps[b][:, :], func=AF.Identity,
                                                    bias=pc(P_PBS + c), scale=pc(P_PS + c)),
                  r=[("ps", b)] + PARK, w=[("scr", tm)])
                E(DVE, lambda: nc.vector.tensor_tensor(out=h[:, c, :], in0=SCR[tm][:, 0:512], in1=hx(c), op=ALU.add),
                  r=[("scr", tm), ("hx", c)], w=[("h", c)])
            return nm2

        def pool_norm(nm2):
            for c in range(8):
                nm2.feed(c)
                nm2.mm(keep=2)
            return nm2.finish()

        PAIRS = [(7, 6), (5, 4), (3, 2), (1, 0)]

        def make_mid(tn):
            def mid():
                ri_ = head_s1(tn)
                gen = head_pool_steps(tn, ri_)

                def adv(n):
                    def f():
                        for _ in range(n):
                            try:
                                next(gen)
                            except StopIteration:
                                return
                    return f
                return [adv(0)] + [adv(4)] * 7 + [adv(1000)]
            return mid

        assert stop_after == "final"
        ri0 = head_s1(0)
        for _ in head_pool_steps(0, ri0):
            pass
        nm2 = pool_mm()
        ri = pool_norm(nm2)
        hn_std(ri, 1)
        for t in range(NT):
            nm3 = Norm()
            ffn(0, t, nm3)
            if stop_i == 2:
                nm3.mm(0)
                emit_out(t)
                for _ in range(NPP - 19):
                    done_piece()
                continue
            ri = nm3.finish()
            hn_std(ri, 2)

            for i in range(2):
                s = get_piece()
                for cc in range(4):
                    c = i * 4 + cc
                    b = nb()
                    for kc in range(KC):
                        E(PE, lambda: nc.tensor.matmul(ps[b][:, :], wslot[s][:, kc * 512 + cc * 128: kc * 512 + cc * 128 + 128],
                                                       hn[:, kc, :], start=(kc == 0), stop=(kc == KC - 1)),
                          r=[("w", s), ("hn", kc)], w=[("ps", b)], sig=(kc == KC - 1))
                    E(ACT, lambda: nc.scalar.activation(out=qT[:, c, :], in_=ps[b][:, :], func=AF.Identity, bias=pc(P_BQ + c)),
                      r=[("ps", b)] + PARK, w=[("q", c), ("hx", c // 2)])
                done_piece()
            s = get_piece()
            kcol = ((4 * t) % 8) * 128
            for g in range(4):
                b = nb()
                for kc in range(KC):
                    E(PE, lambda: nc.tensor.matmul(ps[b][:, :], wslot[s][:, kc * 512 + g * 128: kc * 512 + g * 128 + 128],
                                                   hn[:, kc, :], start=(kc == 0), stop=(kc == KC - 1)),
                      r=[("w", s), ("hn", kc)], w=[("ps", b)], sig=(kc == KC - 1))
                E(ACT, lambda: nc.scalar.activation(out=kTlo[0:64, g, kcol:kcol + 512], in_=ps[b][0:64, :], func=AF.Identity,
                                                    bias=par[0:64, P_BK + g:P_BK + g + 1]),
                  r=[("ps", b)] + PARK, w=[("klo", g)])
                E(ACT, lambda: nc.scalar.activation(out=kThi[64:128, g, kcol:kcol + 512], in_=ps[b][64:128, :], func=AF.Identity,
                                                    bias=par[64:128, P_BK + g:P_BK + g + 1]),
                  r=[("ps", b)] + PARK, w=[("khi", g)])
            done_piece()
            s = get_piece()
            for tb in range(4):
                if tb % 2 == 0:
                    b = nb()
                off = (tb % 2) * 256
                vs = (4 * t + tb) % 5
                for kc in range(KC):
                    E(PE, lambda: nc.tensor.matmul(ps[b][:, off:off + 256], hn[:, kc, tb * 128:(tb + 1) * 128],
                                                   wslot[s][:, kc * 256:(kc + 1) * 256], start=(kc == 0), stop=(kc == KC - 1)),
                      r=[("w", s), ("hn", kc)], w=[("ps", b)], sig=(kc == KC - 1))
                for g in range(4):
                    E(DVE, lambda: nc.vector.tensor_tensor(out=vpad[:, vs, g, 64:128], in0=ps[b][:, off + g * 64: off + g * 64 + 64],
                                                           in1=par[:, P_BV + g * 64: P_BV + g * 64 + 64], op=ALU.add),
                      r=[("ps", b)] + PARK, w=[("v", vs)])
            done_piece()

            kbs = [-1, 0, 1, 2, 3]

            def scores(c):
                g = c // 2
                pt = PT[c % 2]
                groups = []
                if t > 0:
                    groups.append((0, [(-1, 0, 128, 128), (3, 384, 128, 0)]))
                else:
                    groups.append((0, [(3, 384, 128, 0)]))
                for kb in (0, 1, 2):
                    groups.append((kb + 1, [(kb, kb * 128, 256, 0)]))
                for slot, parts in groups:
                    b = nb()
                    mms = [(kb, qc, n, oc, hh) for (kb, qc, n, oc) in parts for hh in range(2)]
                    for mi, (kb, qc, n, oc, hh) in enumerate(mms):
                        kc0 = ((4 * t + kb) % 8) * 128
                        kt = kTlo if hh == 0 else kThi
                        kk = ("klo", g) if hh == 0 else ("khi", g)
                        E(PE, lambda: nc.tensor.matmul(ps[b][:, hh * 256 + oc: hh * 256 + oc + n], kt[:, g, kc0:kc0 + 128],
                                                       qT[:, c, qc:qc + n], start=True, stop=True),
                          r=[kk, ("q", c)], w=[("ps", b)], sig=(mi == len(mms) - 1))
                    ex = scr()
                    if t > 0 or slot != 0:
                        rngs = [(0, 512)]
                    else:
                        rngs = [(0, 128), (256, 128)]
                    me = DVE if slot % 2 == 1 else POOL
                    for (lo, n) in rngs:
                        E(ACT, lambda: nc.scalar.activation(out=SCR[ex][:, lo:lo + n], in_=ps[b][:, lo:lo + n], func=AF.Exp,
                                                            scale=0.125),
                          r=[("ps", b)], w=[("scr", ex)])
                    for (lo, n) in rngs:
                        E(me, lambda: me.h.tensor_tensor(out=pt[:, slot, lo:lo + n], in0=SCR[ex][:, lo:lo + n],
                                                         in1=etab[:, c * 512 + lo: c * 512 + lo + n], op=ALU.mult),
                          r=[("scr", ex), "etab"], w=[("pt", c % 2, slot)])

            def pv(c):
                g = c // 2
                pt = PT[c % 2]
                bo = nb()
                bd = nb()
                for which in range(2):
                    bk = bo if which == 0 else bd
                    for qb in range(4):
                        terms = []
                        if not (t == 0 and qb == 0):
                            terms.append((qb - 1, 128))
                        terms.append((qb, 0))
                        mms = [(kb, po, hh) for (kb, po) in terms for hh in range(2)]
                        for mi, (kb, po, hh) in enumerate(mms):
                            kbi = 0 if kb in (-1, 3) else kb + 1
                            vs = (4 * t + kb) % 5
                            if which == 0:
                                lhs = vpad[:, vs, g, 64:192] if hh == 0 else vpad[:, vs, g, 0:128]
                                rk = [("v", vs)]
                            else:
                                lhs = onespad[:, 64:192] if hh == 0 else onespad[:, 0:128]
                                rk = ["onespad"]
                            E(PE, lambda: nc.tensor.matmul(ps[bk][:, qb * 128:(qb + 1) * 128], lhs,
                                                           pt[:, kbi, hh * 256 + po: hh * 256 + po + 128],
                                                           start=(mi == 0), stop=(mi == len(mms) - 1)),
                              r=rk + [("pt", c % 2, kbi)], w=[("ps", bk)], sig=(mi == len(mms) - 1))
                rc = scr()
                E(ACT, lambda: nc.scalar.activation(out=SCR[rc][:, 0:512], in_=ps[bd][:, :], func=AF.Ln, bias=pc(P_SE + c), scale=1.0),
                  r=[("ps", bd)] + PARK, w=[("scr", rc)])
                E(ACT, lambda: nc.scalar.activation(out=SCR[rc][:, 0:512], in_=SCR[rc][:, 0:512], func=AF.Exp, scale=-1.0),
                  r=[("scr", rc)], w=[("scr", rc)])
                E(DVE, lambda: nc.vector.tensor_tensor(out=oT[:, c, :], in0=ps[bo][:, :], in1=SCR[rc][:, 0:512], op=ALU.mult),
                  r=[("ps", bo), ("scr", rc)], w=[("o", c), ("hx", 4 + c // 2)])

            scores(0)
            for c in range(8):
                if c + 1 < 8:
                    scores(c + 1)
                pv(c)

            nm4 = Norm()
            for i in range(2):
                s = get_piece()
                for dd in range(4):
                    d = i * 4 + dd
                    b = nb()
                    for kc in range(KC):
                        E(PE, lambda: nc.tensor.matmul(ps[b][:, :], wslot[s][:, kc * 512 + dd * 128: kc * 512 + dd * 128 + 128],
                                                       oT[:, kc, :], start=(kc == 0), stop=(kc == KC - 1)),
                          r=[("w", s), ("o", kc)], w=[("ps", b)], sig=(kc == KC - 1))
                    E(DVE, lambda: nc.vector.tensor_tensor(out=h[:, d, :], in0=ps[b][:, :], in1=h[:, d, :], op=ALU.add),
                      r=[("ps", b), ("h", d)], w=[("h", d)])
                    nm4.feed(d)
                    nm4.mm(keep=1)
                done_piece()
            if stop_i == 3:
                nm4.mm(0)
                emit_out(t)
                for _ in range(19):
                    done_piece()
                continue
            ri = nm4.finish()
            hn_std(ri, 3)

            nm5 = Norm()
            ffn(1, t, nm5, mid=(make_mid(t + 1) if t + 1 < NT else None))
            if stop_i == 4:
                nm5.mm(0)
                emit_out(t)
                continue
            ri = nm5.finish()
            for c in range(8):
                scale_norm(DVE, hf(c), [("hf", c), ("act", 2 * c), ("act", 2 * c + 1)], c, 4, ri)
            if t + 1 < NT:
                nm2 = pool_mm()
            emit_out(t)
            if t + 1 < NT:
                ri = pool_norm(nm2)
                hn_std(ri, 1)

        for so in d_out:
            if so.cnt:
                SP.h.wait_ge(so.sem, so.cnt)
    return nc


def _wl(W):
    K, N = W.shape
    kc = K // 128
    return np.ascontiguousarray(W.reshape(kc, 128, N).transpose(1, 0, 2).reshape(128, kc * N))


def _fm(v, n):
    return np.ascontiguousarray(np.asarray(v, np.float32).reshape(n, 128).T)


def _prep(pool_w, pool_b, pool_scale, attn_w_qkv, attn_b_qkv, attn_sinks, attn_w_o,
          norm_mix, norm_ffn, ffn_w_in, ffn_conv_w, ffn_conv_b, ffn_w_out, norm_f):
    f32 = np.float32
    cols = []
    pw = np.asarray(pool_w, f32)[0]
    cols.append(np.concatenate([_wl(pw[g]) for g in range(4)], axis=1))
    wqkv = np.asarray(attn_w_qkv, f32)[0]
    wo = np.asarray(attn_w_o, f32)[0]
    for l in range(2):
        if l == 1:
            cols.append(_wl(wqkv[:, 0:512]))
            cols.append(_wl(wqkv[:, 512:1024]))
            sel = np.concatenate([np.tile(1024 + g * 64 + np.arange(64), 2) for g in range(4)])
            cols.append(_wl(wqkv[:, sel]))
            cols.append(_wl(wqkv[:, 1280:1536]))
            cols.append(_wl(wo[:, 0:512]))
            cols.append(_wl(wo[:, 512:1024]))
        win = np.asarray(ffn_w_in, f32)[l]
        for j in range(11):
            sel = np.concatenate([128 * (2 * j) + np.arange(128), FF + 128 * (2 * j) + np.arange(128),
                                  128 * (2 * j + 1) + np.arange(128), FF + 128 * (2 * j + 1) + np.arange(128)])
            cols.append(_wl(win[:, sel]))
        wout = np.asarray(ffn_w_out, f32)[l]
        for d in range(8):
            cols.append(_wl(wout[:, d * 128:(d + 1) * 128]))
    wall = np.ascontiguousarray(np.concatenate(cols, axis=1))
    assert wall.shape == (128, NW), wall.shape

    par = np.zeros((128, NP), f32)
    gains = [np.asarray(norm_mix, f32)[0], np.asarray(norm_ffn, f32)[0], np.asarray(norm_mix, f32)[1],
             np.asarray(norm_ffn, f32)[1], np.asarray(norm_f, f32)]
    for i, gv in enumerate(gains):
        par[:, P_G + i * 8:P_G + i * 8 + 8] = _fm(gv, 8)
    par[:, P_PB:P_PB + 8] = _fm(np.asarray(pool_b, f32)[0].reshape(-1), 8)
    par[:, P_PS:P_PS + 8] = _fm(np.asarray(pool_scale, f32)[0], 8)
    bqkv = np.asarray(attn_b_qkv, f32)[0]
    par[:, P_BQ:P_BQ + 8] = _fm(bqkv[:1024], 8)
    p = np.arange(128)
    for g in range(4):
        par[:, P_BK + g] = bqkv[1024 + g * 64 + (p % 64)]
    sk = np.asarray(attn_sinks, f32)[0]
    for c in range(8):
        par[:, P_SK + c] = sk[2 * c + p // 64]
    cw = np.asarray(ffn_conv_w, f32)
    cb = np.asarray(ffn_conv_b, f32)
    for l in range(2):
        for k in range(3):
            for ag in range(2):
                o = P_CW + ((l * 3 + k) * 2 + ag) * 22
                par[:, o:o + 22] = _fm(cw[l, k, ag * FF:(ag + 1) * FF], 22)
        for ag in range(2):
            o = P_CB + (l * 2 + ag) * 22
            par[:, o:o + 22] = _fm(cb[l, ag * FF:(ag + 1) * FF], 22)
    par[:, P_BV:P_BV + 256] = bqkv[1280:1536][None, :]
    for g in range(4):
        w = 2 ** (g + 1)
        par[:, P_IC + g * 16:P_IC + g * 16 + 16] = (1.0 / np.minimum(np.arange(16) + 1, w).astype(f32))[None, :]
    return wall, par


def _consts():
    f32 = np.float32
    hh = np.arange(1, 17, dtype=f32)
    slopes = np.exp2(f32(-8.0 / 16) * hh).astype(f32)
    j = np.arange(128)[:, None]
    i = np.arange(128)[None, :]
    etab = np.zeros((128, 16, 256), f32)
    for hd in range(16):
        dc = (i - j).astype(f32)
        etab[:, hd, 0:128] = np.where(j <= i, np.exp(-slopes[hd] * np.maximum(dc, 0.0)), 0.0)
        dp = (i + 128 - j).astype(f32)
        etab[:, hd, 128:256] = np.where(j > i, np.exp(-slopes[hd] * dp), 0.0)
    return np.ascontiguousarray(etab.reshape(128, 4096)), np.eye(128, dtype=f32)


_CACHE = {}


def run(inputs, S, stop_after="final", trace=False):
    x = np.asarray(inputs["x"], np.float32)
    B = x.shape[0]
    wall, par = _prep(**{k: v for k, v in inputs.items() if k != "x"})
    etab, ident = _consts()
    key = (S, stop_after)
    if key not in _CACHE:
        _CACHE[key] = build(S, stop_after)
    nc = _CACHE[key]
    in_maps = [{"x": np.ascontiguousarray(x[b, :S]), "wall": wall, "par": par, "etab": etab, "ident": ident}
               for b in range(B)]
    res = run_bass_kernel_spmd(nc, in_maps, core_ids=list(range(B)), trace=trace)
    out = np.stack([np.asarray(r["out"]) for r in res.results]).astype(np.float32)
    return out, res


def kernel(**inputs):
    out, _ = run(inputs, 4096)
    return out
```
